# Optimizing a Trainium2 kernel written in Bass

```python
import math
import jax, jax.numpy as jnp
from jax import lax
import numpy as np

D_MODEL = 1024
BATCH = 4
SEQ = 8192
DEPTH = 4

FFN_DIM = ((8 * D_MODEL // 3 + 127) // 128) * 128
DN_ALPHA = (2.0 * DEPTH) ** 0.25
DN_BETA = (8.0 * DEPTH) ** -0.25
LN_EPS = 1e-5
CONV_DIM = D_MODEL // 2
CONV_WIDTH = 31
NSA_HEAD_DIM = 64
NSA_HEADS = (D_MODEL // 2) // NSA_HEAD_DIM
NSA_KV_GROUPS = max(1, NSA_HEADS // 4)
NSA_HPG = NSA_HEADS // NSA_KV_GROUPS
CMP_LEN = 32
CMP_STRIDE = 16
CMP_HIDDEN = 128
SEL_BLOCK = 64
SEL_TOPN = 16
WINDOW = 512
Q_BLOCK = 128
FORCE_SCORE = 1e4
NEG_INF = -1e30
EVEN_IN_COLS = 2 * CONV_DIM + NSA_HEADS * NSA_HEAD_DIM + 6 * NSA_KV_GROUPS * NSA_HEAD_DIM + 3 * NSA_HEADS
S5_GROUP = 16
S5_GROUPS = D_MODEL // S5_GROUP
S5_STATE = 64
S5_CHUNK = 128
DT_MIN = 0.001
DT_MAX = 0.1
N_EVEN = (DEPTH + 1) // 2
N_ODD = DEPTH // 2

kernel_name = 'hybrid_conformer_nsa_s5_deepnorm'


def layer_norm(x, g, b):
    xf = x.astype(jnp.float32)
    mu = jnp.mean(xf, axis=-1, keepdims=True)
    var = jnp.mean(jnp.square(xf - mu), axis=-1, keepdims=True)
    return ((xf - mu) * lax.rsqrt(var + LN_EPS) * g + b).astype(x.dtype)


def swiglu(x, w_in, w_out):
    gate, val = jnp.split(x @ w_in, 2, axis=-1)
    return (jax.nn.silu(gate) * val) @ w_out


def masked_softmax(s, mask):
    p = jax.nn.softmax(jnp.where(mask, s, NEG_INF), axis=-1)
    return jnp.where(mask, p, 0.0)


def alibi_slopes():
    h = jnp.arange(1, NSA_HEADS + 1, dtype=jnp.float32)
    return (2.0 ** (-8.0 * h / NSA_HEADS)).reshape(NSA_KV_GROUPS, NSA_HPG)


def causal_depthwise_conv(a, w, bias):
    out = lax.conv_general_dilated(a, w[:, None, :].astype(a.dtype), window_strides=(1,),
                                   padding=[(CONV_WIDTH - 1, 0)],
                                   dimension_numbers=('NWC', 'WIO', 'NWC'),
                                   feature_group_count=a.shape[-1])
    return out + bias


def compress_blocks(t, pe, w1, w2):
    b, s, g, dh = t.shape
    chunks = t.reshape(b, s // CMP_STRIDE, CMP_STRIDE, g, dh)
    blocks = jnp.concatenate([chunks[:, :-1], chunks[:, 1:]], axis=2) + pe[None, None, :, None, :]
    flat = blocks.transpose(0, 1, 3, 2, 4).reshape(b, s // CMP_STRIDE - 1, g, CMP_LEN * dh)
    return jax.nn.silu(flat @ w1) @ w2


def nsa_attention(q, kc, vc, ks, vs, kw, vw, gates):
    b, s = q.shape[0], q.shape[1]
    g_, dh = NSA_KV_GROUPS, NSA_HEAD_DIM
    nc = kc.shape[1]
    nsel = s // SEL_BLOCK
    topn = min(SEL_TOPN, nsel)
    slopes = alibi_slopes()
    cstart = jnp.arange(nc) * CMP_STRIDE
    cend = cstart + CMP_LEN - 1
    sel_ids = jnp.arange(nsel)
    overlap = ((cend[:, None] >= sel_ids[None, :] * SEL_BLOCK)
               & (cstart[:, None] < (sel_ids[None, :] + 1) * SEL_BLOCK)).astype(jnp.float32)
    ks_blocks = ks.reshape(b, nsel, SEL_BLOCK, g_, dh).transpose(0, 3, 1, 2, 4)
    vs_blocks = vs.reshape(b, nsel, SEL_BLOCK, g_, dh).transpose(0, 3, 1, 2, 4)
    kw_pad = jnp.pad(kw, ((0, 0), (WINDOW, 0), (0, 0), (0, 0)))
    vw_pad = jnp.pad(vw, ((0, 0), (WINDOW, 0), (0, 0), (0, 0)))
    b_idx = jnp.arange(b)[:, None, None, None]
    g_idx = jnp.arange(g_)[None, :, None, None]
    win_offsets = jnp.arange(WINDOW + Q_BLOCK) - WINDOW
    tok_offsets = jnp.arange(SEL_BLOCK)

    def query_block(qb):
        t0 = qb * Q_BLOCK
        qpos = t0 + jnp.arange(Q_BLOCK)
        qblk = lax.dynamic_slice_in_dim(q, t0, Q_BLOCK, axis=1)
        gblk = lax.dynamic_slice_in_dim(gates, t0, Q_BLOCK, axis=1)
        dist_c = (qpos[:, None] - cend[None, :]).astype(jnp.float32)
        s_c = jnp.einsum('bqgjd,bcgd->bgjqc', qblk, kc).astype(jnp.float32) \
            - slopes[None, :, :, None, None] * dist_c
        p_c = masked_softmax(s_c, dist_c >= 0)
        o_c = jnp.einsum('bgjqc,bcgd->bqgjd', p_c.astype(vc.dtype), vc)
        imp = jnp.einsum('bgjqc,cn->bgqn', p_c, overlap)
        cur = (qpos // SEL_BLOCK)[:, None]
        valid = sel_ids[None, :] <= cur
        forced = (sel_ids[None, :] == 0) | (sel_ids[None, :] == cur) | (sel_ids[None, :] == cur - 1)
        score = jnp.where(valid, jnp.where(forced, FORCE_SCORE, imp), -1.0)
        top_val, top_idx = lax.top_k(score, topn)
        k_sel = ks_blocks[b_idx, g_idx, top_idx].reshape(b, g_, Q_BLOCK, topn * SEL_BLOCK, dh)
        v_sel = vs_blocks[b_idx, g_idx, top_idx].reshape(b, g_, Q_BLOCK, topn * SEL_BLOCK, dh)
        kpos = (top_idx[..., None] * SEL_BLOCK + tok_offsets).reshape(b, g_, Q_BLOCK, topn * SEL_BLOCK)
        blk_ok = jnp.repeat(top_val >= 0, SEL_BLOCK, axis=-1)
        dist_s = qpos[None, None, :, None] - kpos
        mask_s = (blk_ok & (dist_s >= 0))[:, :, None]
        s_s = jnp.einsum('bqgjd,bgqmd->bgjqm', qblk, k_sel).astype(jnp.float32) \
            - slopes[None, :, :, None, None] * dist_s[:, :, None].astype(jnp.float32)
        p_s = masked_softmax(s_s, mask_s)
        o_s = jnp.einsum('bgjqm,bgqmd->bqgjd', p_s.astype(v_sel.dtype), v_sel)
        k_w = lax.dynamic_slice_in_dim(kw_pad, t0, WINDOW + Q_BLOCK, axis=1)
        v_w = lax.dynamic_slice_in_dim(vw_pad, t0, WINDOW + Q_BLOCK, axis=1)
        wpos = t0 + win_offsets
        dist_w = qpos[:, None] - wpos[None, :]
        mask_w = (dist_w >= 0) & (dist_w < WINDOW) & (wpos[None, :] >= 0)
        s_w = jnp.einsum('bqgjd,bkgd->bgjqk', qblk, k_w).astype(jnp.float32) \
            - slopes[None, :, :, None, None] * dist_w.astype(jnp.float32)
        p_w = masked_softmax(s_w, mask_w)
        o_w = jnp.einsum('bgjqk,bkgd->bqgjd', p_w.astype(v_w.dtype), v_w)
        return gblk[..., 0:1] * o_c + gblk[..., 1:2] * o_s + gblk[..., 2:3] * o_w

    out = lax.map(query_block, jnp.arange(s // Q_BLOCK))
    return out.transpose(1, 0, 2, 3, 4, 5).reshape(b, s, NSA_HEADS * dh)


def conv_nsa_mixer(x, w_in, conv_w, conv_b, cln_g, cln_b, pe_k, w1_k, w2_k, pe_v, w1_v, w2_v, w_out):
    b, s, _ = x.shape
    gd = NSA_KV_GROUPS * NSA_HEAD_DIM
    splits = list(np.cumsum([2 * CONV_DIM, NSA_HEADS * NSA_HEAD_DIM, gd, gd, gd, gd, gd, gd]))
    a_in, q, kc, vc, ks, vs, kw, vw, g = jnp.split(x @ w_in, splits, axis=-1)
    a = a_in[..., :CONV_DIM] * jax.nn.sigmoid(a_in[..., CONV_DIM:])
    a = jax.nn.silu(layer_norm(causal_depthwise_conv(a, conv_w, conv_b), cln_g, cln_b))
    kv_shape = (b, s, NSA_KV_GROUPS, NSA_HEAD_DIM)
    q = q.reshape(b, s, NSA_KV_GROUPS, NSA_HPG, NSA_HEAD_DIM) * (NSA_HEAD_DIM ** -0.5)
    kc = compress_blocks(kc.reshape(kv_shape), pe_k, w1_k, w2_k)
    vc = compress_blocks(vc.reshape(kv_shape), pe_v, w1_v, w2_v)
    gates = jax.nn.sigmoid(g).reshape(b, s, NSA_KV_GROUPS, NSA_HPG, 3)
    o = nsa_attention(q, kc, vc, ks.reshape(kv_shape), vs.reshape(kv_shape),
                      kw.reshape(kv_shape), vw.reshape(kv_shape), gates)
    return jnp.concatenate([a, o], axis=-1) @ w_out


def _complex_affine_combine(e1, e2):
    a1r, a1i, b1r, b1i = e1
    a2r, a2i, b2r, b2i = e2
    return (a2r * a1r - a2i * a1i, a2r * a1i + a2i * a1r,
            a2r * b1r - a2i * b1i + b2r, a2r * b1i + a2i * b1r + b2i)


def s5_mixer(x, a_re, a_im, log_dt, b_re, b_im, c_re, c_im, d_skip, w_glu):
    f32 = jnp.float32
    b, s, dm = x.shape
    u = x.astype(f32).reshape(b, s, S5_GROUPS, S5_GROUP)
    dt = jnp.exp(log_dt.astype(f32))[:, None]
    ar, ai = a_re.astype(f32), a_im.astype(f32)
    mag = jnp.exp(ar * dt)
    lr, li = mag * jnp.cos(ai * dt), mag * jnp.sin(ai * dt)
    den = ar * ar + ai * ai
    zr = ((lr - 1.0) * ar + li * ai) / den
    zi = (li * ar - (lr - 1.0) * ai) / den
    br, bi = b_re.astype(f32), b_im.astype(f32)
    bbr = zr[..., None] * br - zi[..., None] * bi
    bbi = zr[..., None] * bi + zi[..., None] * br
    cr, ci = c_re.astype(f32), c_im.astype(f32)
    n_chunks = s // S5_CHUNK
    u_chunks = u.reshape(b, n_chunks, S5_CHUNK, S5_GROUPS, S5_GROUP).transpose(1, 0, 2, 3, 4)

    def step(carry, uc):
        hr, hi = carry
        bur = jnp.einsum('blgc,gpc->blgp', uc, bbr)
        bui = jnp.einsum('blgc,gpc->blgp', uc, bbi)
        pr, pi, sr, si = lax.associative_scan(
            _complex_affine_combine,
            (jnp.broadcast_to(lr, bur.shape), jnp.broadcast_to(li, bur.shape), bur, bui), axis=1)
        sr = sr + pr * hr[:, None] - pi * hi[:, None]
        si = si + pr * hi[:, None] + pi * hr[:, None]
        y = jnp.einsum('blgp,gcp->blgc', sr, cr) - jnp.einsum('blgp,gcp->blgc', si, ci)
        return (sr[:, -1], si[:, -1]), y

    h0 = jnp.zeros((b, S5_GROUPS, S5_STATE), f32)
    _, y = lax.scan(step, (h0, h0), u_chunks)
    y = y.transpose(1, 0, 2, 3, 4).reshape(b, s, dm) + d_skip.astype(f32) * x.astype(f32)
    val, gate = jnp.split(jax.nn.gelu(y).astype(x.dtype) @ w_glu, 2, axis=-1)
    return val * jax.nn.sigmoid(gate)


def setup_inputs(seed: int = 0) -> dict:
    key = jax.random.key(seed)
    keys = iter(jax.random.split(key, 64))
    f32 = jnp.float32

    def nrm(shape, scale):
        return scale * jax.random.normal(next(keys), shape, f32)

    D, F = D_MODEL, FFN_DIM
    dh, gd = NSA_HEAD_DIM, NSA_KV_GROUPS * NSA_HEAD_DIM
    mix_out = CONV_DIM + NSA_HEADS * NSA_HEAD_DIM
    inp = {}
    inp['x'] = nrm((BATCH, SEQ, D), 1.0)
    inp['ffn1_w_in'] = nrm((DEPTH, D, 2 * F), D ** -0.5)
    inp['ffn1_w_out'] = nrm((DEPTH, F, D), DN_BETA * F ** -0.5)
    inp['ffn2_w_in'] = nrm((DEPTH, D, 2 * F), D ** -0.5)
    inp['ffn2_w_out'] = nrm((DEPTH, F, D), DN_BETA * F ** -0.5)
    inp['ln_g'] = 1.0 + nrm((DEPTH, 3, D), 0.02)
    inp['ln_b'] = nrm((DEPTH, 3, D), 0.02)
    inp['ev_w_in'] = nrm((N_EVEN, D, EVEN_IN_COLS), D ** -0.5)
    inp['ev_conv_w'] = nrm((N_EVEN, CONV_WIDTH, CONV_DIM), CONV_WIDTH ** -0.5)
    inp['ev_conv_b'] = nrm((N_EVEN, CONV_DIM), 0.02)
    inp['ev_cln_g'] = 1.0 + nrm((N_EVEN, CONV_DIM), 0.02)
    inp['ev_cln_b'] = nrm((N_EVEN, CONV_DIM), 0.02)
    inp['ev_pe_k'] = nrm((N_EVEN, CMP_LEN, dh), 0.1)
    inp['ev_w1_k'] = nrm((N_EVEN, CMP_LEN * dh, CMP_HIDDEN), (CMP_LEN * dh) ** -0.5)
    inp['ev_w2_k'] = nrm((N_EVEN, CMP_HIDDEN, dh), CMP_HIDDEN ** -0.5)
    inp['ev_pe_v'] = nrm((N_EVEN, CMP_LEN, dh), 0.1)
    inp['ev_w1_v'] = nrm((N_EVEN, CMP_LEN * dh, CMP_HIDDEN), (CMP_LEN * dh) ** -0.5)
    inp['ev_w2_v'] = nrm((N_EVEN, CMP_HIDDEN, dh), CMP_HIDDEN ** -0.5)
    inp['ev_w_out'] = nrm((N_EVEN, mix_out, D), DN_BETA * mix_out ** -0.5)
    n_idx = jnp.arange(S5_STATE, dtype=f32)
    inp['od_a_re'] = -0.5 * (1.0 + nrm((N_ODD, S5_GROUPS, S5_STATE), 0.01))
    inp['od_a_im'] = math.pi * n_idx + nrm((N_ODD, S5_GROUPS, S5_STATE), 0.01)
    inp['od_log_dt'] = jax.random.uniform(next(keys), (N_ODD, S5_GROUPS), f32,
                                          minval=math.log(DT_MIN), maxval=math.log(DT_MAX))
    inp['od_b_re'] = nrm((N_ODD, S5_GROUPS, S5_STATE, S5_GROUP), (2 * S5_GROUP) ** -0.5)
    inp['od_b_im'] = nrm((N_ODD, S5_GROUPS, S5_STATE, S5_GROUP), (2 * S5_GROUP) ** -0.5)
    inp['od_c_re'] = nrm((N_ODD, S5_GROUPS, S5_GROUP, S5_STATE), (2 * S5_STATE) ** -0.5)
    inp['od_c_im'] = nrm((N_ODD, S5_GROUPS, S5_GROUP, S5_STATE), (2 * S5_STATE) ** -0.5)
    inp['od_d'] = nrm((N_ODD, D), 1.0)
    inp['od_w_glu'] = jnp.concatenate([nrm((N_ODD, D, D), DN_BETA * D ** -0.5),
                                       nrm((N_ODD, D, D), D ** -0.5)], axis=-1)
    return inp


def reference(x, ffn1_w_in, ffn1_w_out, ffn2_w_in, ffn2_w_out, ln_g, ln_b,
              ev_w_in, ev_conv_w, ev_conv_b, ev_cln_g, ev_cln_b,
              ev_pe_k, ev_w1_k, ev_w2_k, ev_pe_v, ev_w1_v, ev_w2_v, ev_w_out,
              od_a_re, od_a_im, od_log_dt, od_b_re, od_b_im, od_c_re, od_c_im, od_d, od_w_glu):
    h = x
    for layer in range(DEPTH):
        i = layer // 2
        h = layer_norm(DN_ALPHA * h + 0.5 * swiglu(h, ffn1_w_in[layer], ffn1_w_out[layer]),
                       ln_g[layer, 0], ln_b[layer, 0])
        if layer % 2 == 0:
            m = conv_nsa_mixer(h, ev_w_in[i], ev_conv_w[i], ev_conv_b[i], ev_cln_g[i], ev_cln_b[i],
                               ev_pe_k[i], ev_w1_k[i], ev_w2_k[i], ev_pe_v[i], ev_w1_v[i], ev_w2_v[i],
                               ev_w_out[i])
        else:
            m = s5_mixer(h, od_a_re[i], od_a_im[i], od_log_dt[i], od_b_re[i], od_b_im[i],
                         od_c_re[i], od_c_im[i], od_d[i], od_w_glu[i])
        h = layer_norm(DN_ALPHA * h + m, ln_g[layer, 1], ln_b[layer, 1])
        h = layer_norm(DN_ALPHA * h + 0.5 * swiglu(h, ffn2_w_in[layer], ffn2_w_out[layer]),
                       ln_g[layer, 2], ln_b[layer, 2])
    return h
```

```python
import math
from contextlib import ExitStack
import numpy as np
import concourse.bass as bass
import concourse.mybir as mybir
from concourse.bass_utils import run_bass_kernel_spmd

F32 = mybir.dt.float32
BF16 = mybir.dt.bfloat16
AF = mybir.ActivationFunctionType
ALU = mybir.AluOpType

D = 1024
FF = 2816
DEPTH = 4
DN_ALPHA = (2.0 * DEPTH) ** 0.25
LN_EPS = 1e-5


_NM = [0]


def SB(nc, es, args):
    name, shape, dt = args
    _NM[0] += 1
    return es.enter_context(nc.sbuf_tensor("%s_%d" % (name, _NM[0]), shape, dt))


class Buf:
    __slots__ = ("name", "w", "r")

    def __init__(self, name=""):
        self.name = name
        self.w = None
        self.r = {}


class Sched:
    ROT = 30000

    def __init__(self, nc, es, n_dma=24):
        self.nc = nc
        self.es = es
        self.engs = {"pe": nc.tensor, "act": nc.scalar, "dve": nc.vector, "pool": nc.gpsimd, "sp": nc.sync}
        self.cur = {}
        self.cnt = {}
        self.nsem = 0
        self.seen = {e: {} for e in self.engs}
        for e in self.engs:
            self._new_sem(e)
        self.dma_sems = [es.enter_context(nc.semaphore("dq%d" % i)) for i in range(n_dma)]
        self.dma_val = [0] * n_dma
        self.dma_slots = {"sp": list(range(0, n_dma - 8)), "pool": list(range(n_dma - 8, n_dma))}
        self.dma_next = {"sp": 0, "pool": 0}
        self.n_inst = 0
        self.n_wait = 0

    def _new_sem(self, e):
        self.cur[e] = self.es.enter_context(self.nc.semaphore("s_%s_%d" % (e, self.nsem)))
        self.nsem += 1
        self.cnt[e] = 0

    def _wait(self, e, ev):
        sem, val = ev
        k = id(sem)
        if self.seen[e].get(k, 0) >= val:
            return
        self.engs[e].wait_ge(sem, val)
        self.seen[e][k] = val
        self.n_wait += 1

    def _deps(self, e, reads, writes):
        own = id(self.cur[e])
        for b in reads:
            if b.w is not None:
                if e == "pe" and id(b.w[0]) == own:
                    continue
                self._wait(e, b.w)
        for b in writes:
            if b.w is not None and not (e == "pe" and id(b.w[0]) == own):
                self._wait(e, b.w)
            for ev in b.r.values():
                if e == "pe" and id(ev[0]) == own:
                    continue
                self._wait(e, ev)

    def _mark(self, ev, reads, writes):
        k = id(ev[0])
        for b in reads:
            b.r[k] = ev
        for b in writes:
            b.w = ev
            b.r = {}

    def op(self, e, fn, reads=(), writes=()):
        self._deps(e, reads, writes)
        if self.cnt[e] >= self.ROT:
            self._new_sem(e)
        inst = fn(self.engs[e])
        self.cnt[e] += 1
        inst.then_inc(self.cur[e], 1)
        ev = (self.cur[e], self.cnt[e])
        self._mark(ev, reads, writes)
        self.n_inst += 1
        return ev

    def dma(self, q, out, in_, reads=(), writes=(), **kw):
        self._deps(q, reads, writes)
        slots = self.dma_slots[q]
        i = slots[self.dma_next[q] % len(slots)]
        self.dma_next[q] += 1
        sem = self.dma_sems[i]
        if self.dma_val[i] > 0:
            self._wait(q, (sem, self.dma_val[i]))
        self.engs[q].dma_start(out=out, in_=in_, **kw).then_inc(sem, 16)
        self.dma_val[i] += 16
        ev = (sem, self.dma_val[i])
        self._mark(ev, reads, writes)
        self.n_inst += 1
        return ev

    def barrier(self):
        evs = [(self.cur[e], self.cnt[e]) for e in self.engs if self.cnt[e] > 0]
        evs += [(s, v) for s, v in zip(self.dma_sems, self.dma_val) if v > 0]
        for e in self.engs:
            for ev in evs:
                if ev[0] is self.cur[e]:
                    continue
                self._wait(e, ev)

    def final_wait(self, e="sp"):
        for s, v in zip(self.dma_sems, self.dma_val):
            if v > 0:
                self._wait(e, (s, v))


class Ctx:
    pass


def setup_ctx(nc, es, ident_ap):
    cx = Ctx()
    cx.nc = nc
    cx.S = Sched(nc, es)
    S = cx.S
    cx.ps = [es.enter_context(nc.psum_tensor("ps%d" % i, [128, 512], F32)) for i in range(8)]
    cx.psb = [Buf("ps%d" % i) for i in range(8)]
    cx.ident = SB(nc, es, ("ident_sb", [128, 128], BF16))
    cx.ident_b = Buf("ident")
    S.dma("pool", cx.ident[:], ident_ap, writes=[cx.ident_b])
    cx.identf = SB(nc, es, ("identf", [128, 128], F32))
    cx.identf_b = Buf("identf")
    S.dma("sp", cx.identf[:], ident_ap, writes=[cx.identf_b])
    return cx


def ln_epilogue_setup(cx, es, g_ap, b_ap):
    nc, S = cx.nc, cx.S
    G = SB(nc, es, ("lnG", [128, D], F32))
    B = SB(nc, es, ("lnB", [128, D], F32))
    gb = Buf("lnG")
    bb = Buf("lnB")
    S.dma("sp", G[:], g_ap.partition_broadcast(128), writes=[gb])
    S.dma("sp", B[:], b_ap.partition_broadcast(128), writes=[bb])
    return (G, gb, B, bb)


class Epi:
    def __init__(self, cx, es, gbt, kres, eps, h_in, h_out, hT_out, tr_banks, nb=2):
        nc = cx.nc
        self.cx = cx
        self.G, self.gb, self.B, self.bb = gbt
        self.kres = kres
        self.eps = eps
        self.h_in, self.h_out, self.hT_out = h_in, h_out, hT_out
        self.tr_banks = tr_banks
        self.nb = nb
        mk = lambda n, sh, dt: [SB(nc, es, ("%s%d" % (n, i), sh, dt)) for i in range(self.nb)]
        self.h = mk("ep_h", [128, D], F32)
        self.r = self.h
        self.xn = self.h
        self.ho = self.h
        self.hb = mk("ep_hb", [128, D], BF16)
        self.st = mk("ep_st", [128, 16], F32)
        self.stage = [SB(nc, es, ("ep_stage%d" % i, [128, 8, 512], BF16)) for i in range(1)]
        self.b = {n: [Buf(n + str(i)) for i in range(self.nb)] for n in ("h", "hb", "st")}
        for n in ("r", "xn", "ho"):
            self.b[n] = self.b["h"]
        self.stage_b = [Buf("stage%d" % i) for i in range(1)]
        self.k = 0

    def prefetch_h(self, tok0):
        cx = self.cx
        i = self.k % self.nb
        cx.S.dma("sp", self.h[i][:], self.h_in[tok0:tok0 + 128, :], writes=[self.b["h"][i]])

    def run(self, tok0, y_ap, y_bufs, pending):
        cx = self.cx
        S = cx.S
        i = self.k % self.nb
        self.k += 1
        b = self.b
        h, r, xn, ho, hb, st = self.h[i], self.r[i], self.xn[i], self.ho[i], self.hb[i], self.st[i]
        S.op("dve", lambda e: e.scalar_tensor_tensor(out=r[:], in0=h[:], scalar=self.kres, in1=y_ap,
                                                     op0=ALU.mult, op1=ALU.add),
             reads=[b["h"][i]] + list(y_bufs), writes=[b["r"][i]])
        self.k -= 1
        self.run_post(tok0, pending)

    def run_post(self, tok0, pending):
        cx = self.cx
        S = cx.S
        i = self.k % self.nb
        self.k += 1
        b = self.b
        h, r, xn, ho, hb, st = self.h[i], self.r[i], self.xn[i], self.ho[i], self.hb[i], self.st[i]
        S.op("dve", lambda e: e.bn_stats(out=st[:, 0:6], in_=r[:, 0:512]), reads=[b["r"][i]], writes=[b["st"][i]])
        S.op("dve", lambda e: e.bn_stats(out=st[:, 6:12], in_=r[:, 512:1024]), reads=[b["r"][i]], writes=[b["st"][i]])
        S.op("dve", lambda e: e.bn_aggr(out=st[:, 12:14], in_=st[:, 0:12]), reads=[b["st"][i]], writes=[b["st"][i]])
        S.op("dve", lambda e: e.tensor_scalar(out=st[:, 14:15], in0=st[:, 13:14], scalar1=self.eps, scalar2=None,
                                              op0=ALU.add), reads=[b["st"][i]], writes=[b["st"][i]])
        S.op("act", lambda e: e.activation(out=st[:, 14:15], in_=st[:, 14:15], func=AF.Sqrt),
             reads=[b["st"][i]], writes=[b["st"][i]])
        S.op("dve", lambda e: e.reciprocal(out=st[:, 14:15], in_=st[:, 14:15]), reads=[b["st"][i]], writes=[b["st"][i]])
        S.op("dve", lambda e: e.tensor_scalar(out=st[:, 15:16], in0=st[:, 12:13], scalar1=-1.0, scalar2=st[:, 14:15],
                                              op0=ALU.mult, op1=ALU.mult), reads=[b["st"][i]], writes=[b["st"][i]])
        S.op("act", lambda e: e.activation(out=xn[:], in_=r[:], func=AF.Identity, scale=st[:, 14:15], bias=st[:, 15:16]),
             reads=[b["r"][i], b["st"][i]], writes=[b["xn"][i]])
        S.op("dve", lambda e: e.tensor_tensor(out=xn[:], in0=xn[:], in1=self.G[:], op=ALU.mult),
             reads=[self.gb], writes=[b["xn"][i]])
        S.op("dve", lambda e: e.tensor_tensor(out=ho[:], in0=xn[:], in1=self.B[:], op=ALU.add),
             reads=[b["xn"][i], self.bb], writes=[b["ho"][i]])
        S.dma("sp", self.h_out[tok0:tok0 + 128, :], ho[:], reads=[b["ho"][i]])
        if self.hT_out is None:
            return
        S.op("act", lambda e: e.copy(out=hb[:], in_=ho[:]), reads=[b["ho"][i]], writes=[b["hb"][i]])
        J, tt = divmod(tok0 // 128, 4)
        sidx = 0
        stage = self.stage[sidx]
        bank = self.tr_banks[(self.k - 1) % len(self.tr_banks)]

        def pe_work():
            trv = cx.ps[bank][:].bitcast(BF16)
            for c in range(8):
                S.op("pe", lambda e, c=c: e.transpose(out=trv[:, c * 128:(c + 1) * 128], in_=hb[:, c * 128:(c + 1) * 128],
                                                      identity=cx.ident[:]),
                     reads=[b["hb"][i], cx.ident_b], writes=[cx.psb[bank]])
            S.op("act", lambda e: e.copy(out=stage[:, :, tt * 128:(tt + 1) * 128],
                                         in_=trv.rearrange("p (c t) -> p c t", c=8)),
                 reads=[cx.psb[bank]], writes=[self.stage_b[sidx]])
            if tt == 3:
                S.dma("sp", self.hT_out[J], stage[:], reads=[self.stage_b[sidx]])

        pending.append(pe_work)


def flush(pending, keep=0):
    while len(pending) > keep:
        pending.pop(0)()


def cast_ffn_weights(cx, w_in, w_out, scr):
    S = cx.S
    S.dma("pool", scr[0].rearrange("r (a b) -> (r a) b", b=1408), w_in.rearrange("r (a b) -> (r a) b", b=1408))
    S.dma("pool", scr[1], w_out)


def ffn_phase(cx, h_in, hT_in, wscr, g_ap, b_ap, h_out, hT_out, T):
    nc, S = cx.nc, cx.S
    NT = T // 512
    with ExitStack() as es:
        Win = SB(nc, es, ("Win", [128, 8, 2 * FF], BF16))
        Wout = SB(nc, es, ("Wout", [128, 22, D], BF16))
        win_b = [Buf("Win%d" % c) for c in range(11)]
        wout_b = Buf("Wout")
        w_in_v = wscr[0].rearrange("(c p) f -> p c f", p=128)
        for blk in range(11):
            for off in (0, FF):
                S.dma("sp" if blk % 2 == 0 else "pool", Win[:, :, off + blk * 256:off + (blk + 1) * 256],
                      w_in_v[:, :, off + blk * 256:off + (blk + 1) * 256], writes=[win_b[blk]])
        w_out_v = wscr[1].rearrange("(c p) d -> p c d", p=128)
        for c0 in range(0, 22, 2):
            S.dma("pool", Wout[:, c0:c0 + 2, :], w_out_v[:, c0:c0 + 2, :], writes=[wout_b])
        gbt = ln_epilogue_setup(cx, es, g_ap, b_ap)
        epi = Epi(cx, es, gbt, DN_ALPHA / 0.5, LN_EPS / 0.25, h_in, h_out, hT_out, tr_banks=[6, 7])
        hT = [SB(nc, es, ("hTsb%d" % i, [128, 8, 512], BF16)) for i in range(2)]
        hT_b = [Buf("hT%d" % i) for i in range(2)]
        actT = SB(nc, es, ("actT", [128, 22, 512], BF16))
        act_b = [Buf("actT%d" % i) for i in range(22)]
        sg = [SB(nc, es, ("sg%d" % i, [128, 512], BF16)) for i in range(2)]
        sg_b = [Buf("sg%d" % i) for i in range(2)]
        pending = []
        S.dma("sp", hT[0][:], hT_in[0], writes=[hT_b[0]])
        for J in range(NT):
            cur = J % 2
            if J + 1 < NT:
                S.dma("sp", hT[1 - cur][:], hT_in[J + 1], writes=[hT_b[1 - cur]])
            for fc in range(22):
                pg, pv = fc % 2, 2 + fc % 2
                for c in range(8):
                    S.op("pe", lambda e, c=c: e.matmul(cx.ps[pg][:], lhsT=Win[:, c, fc * 128:(fc + 1) * 128],
                                                       rhs=hT[cur][:, c, :], start=(c == 0), stop=(c == 7)),
                         reads=[win_b[fc // 2], hT_b[cur]], writes=[cx.psb[pg]])
                for c in range(8):
                    S.op("pe", lambda e, c=c: e.matmul(cx.ps[pv][:], lhsT=Win[:, c, FF + fc * 128:FF + (fc + 1) * 128],
                                                       rhs=hT[cur][:, c, :], start=(c == 0), stop=(c == 7)),
                         reads=[win_b[fc // 2], hT_b[cur]], writes=[cx.psb[pv]])
                k = fc % 2
                S.op("act", lambda e: e.activation(out=sg[k][:], in_=cx.ps[pg][:], func=AF.Silu),
                     reads=[cx.psb[pg]], writes=[sg_b[k]])
                S.op("dve", lambda e: e.tensor_tensor(out=actT[:, fc, :], in0=cx.ps[pv][:], in1=sg[k][:], op=ALU.mult),
                     reads=[cx.psb[pv], sg_b[k]], writes=[act_b[fc]])
                if fc == 3:
                    flush(pending)
            for tt in range(4):
                tok0 = J * 512 + tt * 128
                epi.prefetch_h(tok0)
                for half in range(2):
                    bank = 4 + half
                    for fc in range(22):
                        S.op("pe", lambda e, fc=fc: e.matmul(cx.ps[bank][:], lhsT=actT[:, fc, tt * 128:(tt + 1) * 128],
                                                             rhs=Wout[:, fc, half * 512:(half + 1) * 512],
                                                             start=(fc == 0), stop=(fc == 21)),
                             reads=[act_b[fc], wout_b], writes=[cx.psb[bank]])
                flush(pending)
                _epi_from_banks(cx, epi, tok0, pending)
        flush(pending)
    S.barrier()


def _epi_from_banks(cx, epi, tok0, pending):
    S = cx.S
    i = epi.k % epi.nb
    r, h = epi.r[i], epi.h[i]
    for half in range(2):
        bank = 4 + half
        sl = slice(half * 512, (half + 1) * 512)
        S.op("dve", lambda e: e.scalar_tensor_tensor(out=r[:, sl], in0=h[:, sl], scalar=epi.kres, in1=cx.ps[bank][:],
                                                     op0=ALU.mult, op1=ALU.add),
             reads=[epi.b["h"][i], cx.psb[bank]], writes=[epi.b["r"][i]])
    epi.run_post(tok0, pending)


C_IDENT, C_TRI, C_SEL, C_MASK, C_SP1, C_NSP1, C_END = 0, 128, 256, 384, 896, 897, 898
TWO_PI = 2.0 * math.pi


def make_consts():
    c = np.zeros((128, C_END), np.float32)
    c[:, C_IDENT:C_IDENT + 128] = np.eye(128)
    s = np.arange(128)
    c[:, C_TRI:C_TRI + 128] = (s[:, None] <= s[None, :])
    c[127, C_SEL:C_SEL + 128] = 1.0
    for j in range(4):
        m = np.zeros((128, 128), np.float32)
        for g2 in range(2):
            gl = 2 * j + g2
            m[g2 * 64:(g2 + 1) * 64, gl * 16:(gl + 1) * 16] = 1.0
        c[:, C_MASK + j * 128:C_MASK + (j + 1) * 128] = m
    c[:, C_SP1] = s + 1
    c[:, C_NSP1] = -(s + 1.0)
    return c


def pipeline(stages, items):
    n, m = len(items), len(stages)
    for tau in range(n + m - 1):
        for s in reversed(range(m)):
            i = tau - s
            if 0 <= i < n:
                stages[s](items[i])


def sincos(cx, es, theta, tb, F, sin_out, cos_out, ob):
    nc, S = cx.nc, cx.S
    q = SB(nc, es, ("sc_q", [128, F], F32))
    qi = SB(nc, es, ("sc_qi", [128, F], mybir.dt.int32))
    qb, qib = Buf("q"), Buf("qi")
    for shift, out in ((0.0, sin_out), (0.5 * math.pi, cos_out)):
        S.op("dve", lambda e: e.tensor_scalar(out=q[:], in0=theta, scalar1=shift, scalar2=1.0 / TWO_PI, op0=ALU.add, op1=ALU.mult),
             reads=[tb], writes=[qb])
        S.op("dve", lambda e: e.tensor_copy(out=qi[:], in_=q[:]), reads=[qb], writes=[qib])
        S.op("dve", lambda e: e.tensor_copy(out=q[:], in_=qi[:]), reads=[qib], writes=[qb])
        S.op("dve", lambda e: e.tensor_scalar(out=q[:], in0=q[:], scalar1=-TWO_PI, scalar2=shift, op0=ALU.mult, op1=ALU.add),
             reads=[qb], writes=[qb])
        S.op("dve", lambda e: e.tensor_tensor(out=q[:], in0=q[:], in1=theta, op=ALU.add), reads=[qb, tb], writes=[qb])
        S.op("dve", lambda e: e.tensor_scalar(out=q[:], in0=q[:], scalar1=-math.pi, scalar2=math.pi, op0=ALU.max, op1=ALU.min),
             reads=[qb], writes=[qb])
        S.op("act", lambda e, out=out: e.activation(out=out, in_=q[:], func=AF.Sin), reads=[qb], writes=[ob])


def s5_phase(cx, hT_in, P, consts, gT_out, T, after_setup=None):
    nc, S = cx.nc, cx.S
    NCH = T // 128
    with ExitStack() as es:
        Lr = SB(nc, es, ("Lr", [128, 8, 512], F32))
        Li = SB(nc, es, ("Li", [128, 8, 512], F32))
        Linr = SB(nc, es, ("Linr", [128, 8, 512], BF16))
        Lini = SB(nc, es, ("Lini", [128, 8, 512], BF16))
        MA = SB(nc, es, ("MA", [128, 8, 1024], BF16))
        CP = SB(nc, es, ("CP", [128, 8, 8, 128], BF16))
        dsk = SB(nc, es, ("dsk", [128, 8], F32))
        tri = SB(nc, es, ("tri", [128, 128], BF16))
        self_ = SB(nc, es, ("self", [128, 128], F32))
        msk = SB(nc, es, ("msk", [128, 512], F32))
        sp1 = SB(nc, es, ("sp1", [128, 2], F32))
        tab_b, ma_b, cp_b, cst_b = Buf("tab"), Buf("MA"), Buf("CP"), Buf("cst")
        S.dma("pool", tri[:], consts[:, C_TRI:C_TRI + 128], writes=[cst_b])
        S.dma("sp", self_[:], consts[:, C_SEL:C_SEL + 128], writes=[cst_b])
        S.dma("sp", msk[:], consts[:, C_MASK:C_MASK + 512], writes=[cst_b])
        S.dma("sp", sp1[:], consts[:, C_SP1:C_SP1 + 2], writes=[cst_b])
        with nc.allow_non_contiguous_dma(reason="tiny parameter vectors"):
            S.dma("sp", dsk[:], P["d"].rearrange("(k p) -> p k", p=128), writes=[cst_b], allow_slow_non_contiguous=True)
        if after_setup is not None:
            after_setup()
        with ExitStack() as ts:
            AR = SB(nc, ts, ("AR", [128, 4096], F32))
            AI = SB(nc, ts, ("AI", [128, 4096], F32))
            DT = SB(nc, ts, ("DT", [128, 64], F32))
            MG = SB(nc, ts, ("MG", [128, 4096], F32))
            SN = SB(nc, ts, ("SN", [128, 4096], F32))
            CS = SB(nc, ts, ("CS", [128, 4096], F32))
            ab, ib, db, mb, sb_ = Buf("AR"), Buf("AI"), Buf("DT"), Buf("MG"), Buf("SC")
            S.dma("sp", AR[:], P["a_re"].rearrange("g p -> (g p)").partition_broadcast(128), writes=[ab])
            S.dma("sp", AI[:], P["a_im"].rearrange("g p -> (g p)").partition_broadcast(128), writes=[ib])
            S.dma("sp", DT[:], P["log_dt"].partition_broadcast(128), writes=[db])
            S.op("act", lambda e: e.activation(out=DT[:], in_=DT[:], func=AF.Exp), reads=[db], writes=[db])
            dtb = DT[:].unsqueeze(2).to_broadcast([128, 64, 64])
            v3 = lambda t: t[:].rearrange("p (g q) -> p g q", q=64)
            S.op("dve", lambda e: e.tensor_tensor(out=v3(AR), in0=v3(AR), in1=dtb, op=ALU.mult), reads=[ab, db], writes=[ab])
            S.op("dve", lambda e: e.tensor_tensor(out=v3(AI), in0=v3(AI), in1=dtb, op=ALU.mult), reads=[ib, db], writes=[ib])
            S.op("dve", lambda e: e.tensor_scalar(out=AI[:], in0=AI[:], scalar1=sp1[:, 0:1], scalar2=None, op0=ALU.mult),
                 reads=[ib, cst_b], writes=[ib])
            sincos(cx, ts, AI[:], ib, 4096, SN[:], CS[:], sb_)
            fl = lambda t: t[:].rearrange("p k q -> p (k q)")
            S.op("act", lambda e: e.activation(out=MG[:], in_=AR[:], func=AF.Exp, scale=sp1[:, 0:1]), reads=[ab, cst_b], writes=[mb])
            S.op("dve", lambda e: e.tensor_tensor(out=fl(Lr), in0=MG[:], in1=CS[:], op=ALU.mult), reads=[mb, sb_], writes=[tab_b])
            S.op("dve", lambda e: e.tensor_tensor(out=fl(Li), in0=MG[:], in1=SN[:], op=ALU.mult), reads=[mb, sb_], writes=[tab_b])
            S.op("act", lambda e: e.activation(out=MG[:], in_=AR[:], func=AF.Exp, scale=sp1[:, 1:2]), reads=[ab, cst_b], writes=[mb])
            S.op("dve", lambda e: e.tensor_tensor(out=fl(Linr), in0=MG[:], in1=CS[:], op=ALU.mult), reads=[mb, sb_], writes=[tab_b])
            S.op("dve", lambda e: e.scalar_tensor_tensor(out=fl(Lini), in0=MG[:], scalar=-1.0, in1=SN[:], op0=ALU.mult, op1=ALU.mult),
                 reads=[mb, sb_], writes=[tab_b])
            S.barrier()
        with ExitStack() as ts:
            a2 = SB(nc, ts, ("a2", [128, 8, 32], F32))
            sc = SB(nc, ts, ("sc2", [128, 2, 32], F32))
            BR = SB(nc, ts, ("BR", [128, 32, 16], F32))
            BI = SB(nc, ts, ("BI", [128, 32, 16], F32))
            BBR = SB(nc, ts, ("BBR", [128, 32, 16], F32))
            BBI = SB(nc, ts, ("BBI", [128, 32, 16], F32))
            TM = SB(nc, ts, ("TM", [128, 32, 16], F32))
            bbP = [SB(nc, ts, ("bbP%d" % i, [128, 128], F32)) for i in range(2)]
            CC = [SB(nc, ts, ("CC%d" % i, [128, 8, 2, 64], F32)) for i in range(2)]
            a2b, scb, brb, bbb, tmb, ccb = Buf("a2"), Buf("sc"), Buf("BR"), Buf("BB"), Buf("TM"), Buf("CC")
            bbPb = [Buf("bbP0"), Buf("bbP1")]
            with nc.allow_non_contiguous_dma(reason="tiny parameter vectors"):
                S.dma("sp", a2[:, 0, :], P["a_re"].rearrange("g p -> (g p)").rearrange("(t q) -> q t", q=128), writes=[a2b],
                      allow_slow_non_contiguous=True)
                S.dma("sp", a2[:, 1, :], P["a_im"].rearrange("g p -> (g p)").rearrange("(t q) -> q t", q=128), writes=[a2b],
                      allow_slow_non_contiguous=True)
                ldv = P["log_dt"].rearrange("(t two) -> two t", two=2)
                for g2 in range(2):
                    S.dma("sp", a2[g2 * 64:(g2 + 1) * 64, 2, :], ldv[g2].partition_broadcast(64), writes=[a2b],
                          allow_slow_non_contiguous=True)
            S.dma("sp", BR[:], P["b_re"].rearrange("g p c -> (g p) c").rearrange("(t q) c -> q t c", q=128), writes=[brb])
            S.dma("sp", BI[:], P["b_im"].rearrange("g p c -> (g p) c").rearrange("(t q) c -> q t c", q=128), writes=[brb])
            for i, nm in enumerate(("c_re", "c_im")):
                cv = P[nm].rearrange("(k gl) c p -> (gl c) k p", gl=8)
                for dup in range(2):
                    S.dma("sp", CC[i][:, :, dup, :], cv, writes=[ccb])
            A = lambda i: a2[:, i, :]
            ops = S.op
            ops("act", lambda e: e.activation(out=A(2), in_=A(2), func=AF.Exp), reads=[a2b], writes=[a2b])
            ops("dve", lambda e: e.tensor_tensor(out=A(3), in0=A(0), in1=A(2), op=ALU.mult), reads=[a2b], writes=[a2b])
            ops("dve", lambda e: e.tensor_tensor(out=A(4), in0=A(1), in1=A(2), op=ALU.mult), reads=[a2b], writes=[a2b])
            ops("act", lambda e: e.activation(out=A(3), in_=A(3), func=AF.Exp), reads=[a2b], writes=[a2b])
            sincos(cx, ts, A(4), a2b, 32, sc[:, 0, :], sc[:, 1, :], scb)
            ops("dve", lambda e: e.tensor_tensor(out=A(5), in0=A(3), in1=sc[:, 0, :], op=ALU.mult), reads=[a2b, scb], writes=[a2b])
            ops("dve", lambda e: e.tensor_tensor(out=A(3), in0=A(3), in1=sc[:, 1, :], op=ALU.mult), reads=[a2b, scb], writes=[a2b])
            ops("dve", lambda e: e.tensor_scalar(out=A(3), in0=A(3), scalar1=-1.0, scalar2=None, op0=ALU.add), reads=[a2b], writes=[a2b])
            ops("dve", lambda e: e.tensor_tensor(out=A(2), in0=A(0), in1=A(0), op=ALU.mult), reads=[a2b], writes=[a2b])
            ops("dve", lambda e: e.tensor_tensor(out=A(4), in0=A(1), in1=A(1), op=ALU.mult), reads=[a2b], writes=[a2b])
            ops("dve", lambda e: e.tensor_tensor(out=A(2), in0=A(2), in1=A(4), op=ALU.add), reads=[a2b], writes=[a2b])
            ops("dve", lambda e: e.reciprocal(out=A(2), in_=A(2)), reads=[a2b], writes=[a2b])
            ops("dve", lambda e: e.tensor_tensor(out=A(6), in0=A(3), in1=A(0), op=ALU.mult), reads=[a2b], writes=[a2b])
            ops("dve", lambda e: e.tensor_tensor(out=A(4), in0=A(5), in1=A(1), op=ALU.mult), reads=[a2b], writes=[a2b])
            ops("dve", lambda e: e.tensor_tensor(out=A(6), in0=A(6), in1=A(4), op=ALU.add), reads=[a2b], writes=[a2b])
            ops("dve", lambda e: e.tensor_tensor(out=A(6), in0=A(6), in1=A(2), op=ALU.mult), reads=[a2b], writes=[a2b])
            ops("dve", lambda e: e.tensor_tensor(out=A(7), in0=A(5), in1=A(0), op=ALU.mult), reads=[a2b], writes=[a2b])
            ops("dve", lambda e: e.tensor_tensor(out=A(4), in0=A(3), in1=A(1), op=ALU.mult), reads=[a2b], writes=[a2b])
            ops("dve", lambda e: e.tensor_tensor(out=A(7), in0=A(7), in1=A(4), op=ALU.subtract), reads=[a2b], writes=[a2b])
            ops("dve", lambda e: e.tensor_tensor(out=A(7), in0=A(7), in1=A(2), op=ALU.mult), reads=[a2b], writes=[a2b])
            zrb = a2[:, 6, :].unsqueeze(2).to_broadcast([128, 32, 16])
            zib = a2[:, 7, :].unsqueeze(2).to_broadcast([128, 32, 16])
            ops("dve", lambda e: e.tensor_tensor(out=BBR[:], in0=BR[:], in1=zrb, op=ALU.mult), reads=[a2b, brb], writes=[bbb])
            ops("dve", lambda e: e.tensor_tensor(out=TM[:], in0=BI[:], in1=zib, op=ALU.mult), reads=[a2b, brb], writes=[tmb])
            ops("dve", lambda e: e.tensor_tensor(out=BBR[:], in0=BBR[:], in1=TM[:], op=ALU.subtract), reads=[tmb], writes=[bbb])
            ops("dve", lambda e: e.tensor_tensor(out=BBI[:], in0=BI[:], in1=zrb, op=ALU.mult), reads=[a2b, brb], writes=[bbb])
            ops("dve", lambda e: e.tensor_tensor(out=TM[:], in0=BR[:], in1=zib, op=ALU.mult), reads=[a2b, brb, bbb], writes=[tmb])
            ops("dve", lambda e: e.tensor_tensor(out=BBI[:], in0=BBI[:], in1=TM[:], op=ALU.add), reads=[tmb], writes=[bbb])
            n = 0
            for t in range(32):
                k, j = divmod(t, 4)
                mj = msk[:, j * 128:(j + 1) * 128].rearrange("p (g c) -> p g c", c=16)
                for ri, BB in enumerate((BBR, BBI)):
                    i = n % 2
                    n += 1
                    bank = i
                    src = BB[:, t, :].unsqueeze(1).to_broadcast([128, 8, 16])
                    ops("dve", lambda e: e.tensor_tensor(out=bbP[i][:].rearrange("p (g c) -> p g c", c=16), in0=mj, in1=src, op=ALU.mult),
                        reads=[bbb, cst_b], writes=[bbPb[i]])
                    ops("pe", lambda e: e.transpose(out=cx.ps[bank][:, 0:128], in_=bbP[i][:], identity=cx.identf[:]),
                        reads=[bbPb[i], cx.identf_b], writes=[cx.psb[bank]])
                    ops("act", lambda e: e.copy(out=MA[:, k, ri * 512 + j * 128: ri * 512 + (j + 1) * 128], in_=cx.ps[bank][:, 0:128]),
                        reads=[cx.psb[bank]], writes=[ma_b])
            for k in range(8):
                for ci in range(2):
                    bank = 2 + ci
                    ops("pe", lambda e: e.transpose(out=cx.ps[bank][:, 0:128], in_=CC[ci][:, k].rearrange("p a q -> p (a q)"),
                                                    identity=cx.identf[:]),
                        reads=[ccb, cx.identf_b], writes=[cx.psb[bank]])
                    for j in range(4):
                        ops("dve", lambda e: e.scalar_tensor_tensor(out=CP[:, k, ci * 4 + j, :], in0=cx.ps[bank][:, 0:128],
                                                                    scalar=(1.0 if ci == 0 else -1.0),
                                                                    in1=msk[:, j * 128:(j + 1) * 128], op0=ALU.mult, op1=ALU.mult),
                            reads=[cx.psb[bank], cst_b], writes=[cp_b])
            S.barrier()
        uT = [SB(nc, es, ("s5uT%d" % i, [128, 8, 512], BF16)) for i in range(2)]
        uT_b = [Buf("uT0"), Buf("uT1")]
        M1 = [SB(nc, es, ("M1_%d" % i, [128, 1024], BF16)) for i in range(2)]
        M2 = [SB(nc, es, ("M2_%d" % i, [128, 1024], BF16)) for i in range(2)]
        M_b = [Buf("M0"), Buf("M1")]
        X = SB(nc, es, ("X", [128, 8, 1024], F32))
        X_b = [Buf("X%d" % k) for k in range(8)]
        N2 = [SB(nc, es, ("N2_%d" % i, [128, 1024], F32)) for i in range(2)]
        N2_b = [Buf("N2a"), Buf("N2b")]
        XTs = [SB(nc, es, ("XTs%d" % i, [128, 1024], BF16)) for i in range(2)]
        XTs_b = [Buf("XTs0"), Buf("XTs1")]
        vt = [SB(nc, es, ("vt%d" % i, [128, 128], F32)) for i in range(2)]
        vt_b = [Buf("vt0"), Buf("vt1")]
        GTs = [SB(nc, es, ("GTs%d" % i, [128, 8, 512], BF16)) for i in range(2)]
        GTs_b = [Buf("GT0"), Buf("GT1")]
        BU, W, XT, YT = (0, 1), (2, 3), (4, 5), 6
        S.dma("sp", uT[0][:], hT_in[0], writes=[uT_b[0]])
        items = [(j, k) for j in range(NCH) for k in range(8)]

        def st0(it):
            j, k = it
            J, tt = divmod(j, 4)
            if k == 0 and tt == 1 and J + 1 < T // 512:
                S.dma("sp", uT[(J + 1) % 2][:], hT_in[J + 1], writes=[uT_b[(J + 1) % 2]])
            u = uT[J % 2][:, k, tt * 128:(tt + 1) * 128]
            for ri in range(2):
                S.op("pe", lambda e: e.matmul(cx.ps[BU[ri]][:], lhsT=u, rhs=MA[:, k, ri * 512:(ri + 1) * 512], start=True, stop=True),
                     reads=[uT_b[J % 2], ma_b], writes=[cx.psb[BU[ri]]])

        def st1(it):
            j, k = it
            i = (j * 8 + k) % 2
            bur, bui = cx.ps[BU[0]][:], cx.ps[BU[1]][:]
            rd = [tab_b]
            S.op("dve", lambda e: e.tensor_tensor(out=M1[i][:, 0:512], in0=bur, in1=Linr[:, k, :], op=ALU.mult),
                 reads=rd + [cx.psb[BU[0]]], writes=[M_b[i]])
            S.op("dve", lambda e: e.tensor_tensor(out=M2[i][:, 512:1024], in0=bur, in1=Lini[:, k, :], op=ALU.mult),
                 reads=rd + [cx.psb[BU[0]]], writes=[M_b[i]])
            S.op("dve", lambda e: e.tensor_tensor(out=M1[i][:, 512:1024], in0=bui, in1=Linr[:, k, :], op=ALU.mult),
                 reads=rd + [cx.psb[BU[1]]], writes=[M_b[i]])
            S.op("dve", lambda e: e.scalar_tensor_tensor(out=M2[i][:, 0:512], in0=bui, scalar=-1.0, in1=Lini[:, k, :],
                                                         op0=ALU.mult, op1=ALU.mult),
                 reads=rd + [cx.psb[BU[1]]], writes=[M_b[i]])

        def st2(it):
            j, k = it
            i = (j * 8 + k) % 2
            for ri in range(2):
                sl = slice(ri * 512, (ri + 1) * 512)
                S.op("pe", lambda e: e.matmul(cx.ps[W[ri]][:], lhsT=tri[:], rhs=M1[i][:, sl], start=True, stop=False),
                     reads=[M_b[i], cst_b], writes=[cx.psb[W[ri]]])
                S.op("pe", lambda e: e.matmul(cx.ps[W[ri]][:], lhsT=tri[:], rhs=M2[i][:, sl], start=False, stop=(j == 0)),
                     reads=[M_b[i], cst_b], writes=[cx.psb[W[ri]]])
                if j > 0:
                    S.op("pe", lambda e: e.matmul(cx.ps[W[ri]][:], lhsT=self_[:], rhs=X[:, k, sl], start=False, stop=True),
                         reads=[X_b[k], cst_b], writes=[cx.psb[W[ri]]])

        def st3(it):
            j, k = it
            i = (j * 8 + k) % 2
            wr, wi = cx.ps[W[0]][:], cx.ps[W[1]][:]
            S.op("dve", lambda e: e.tensor_tensor(out=X[:, k, 0:512], in0=wr, in1=Lr[:, k, :], op=ALU.mult),
                 reads=[tab_b, cx.psb[W[0]]], writes=[X_b[k]])
            S.op("dve", lambda e: e.tensor_tensor(out=N2[i][:, 512:1024], in0=wr, in1=Li[:, k, :], op=ALU.mult),
                 reads=[tab_b, cx.psb[W[0]]], writes=[N2_b[i]])
            S.op("dve", lambda e: e.tensor_tensor(out=X[:, k, 512:1024], in0=wi, in1=Lr[:, k, :], op=ALU.mult),
                 reads=[tab_b, cx.psb[W[1]]], writes=[X_b[k]])
            S.op("dve", lambda e: e.scalar_tensor_tensor(out=N2[i][:, 0:512], in0=wi, scalar=-1.0, in1=Li[:, k, :],
                                                         op0=ALU.mult, op1=ALU.mult),
                 reads=[tab_b, cx.psb[W[1]]], writes=[N2_b[i]])
            S.op("pool", lambda e: e.tensor_tensor(out=X[:, k, :], in0=X[:, k, :], in1=N2[i][:], op=ALU.add),
                 reads=[N2_b[i]], writes=[X_b[k]])

        def st4(it):
            j, k = it
            for b in range(8):
                bank = XT[b // 4]
                S.op("pe", lambda e: e.transpose(out=cx.ps[bank][:, (b % 4) * 128:(b % 4 + 1) * 128], in_=X[:, k, b * 128:(b + 1) * 128],
                                                 identity=cx.identf[:]),
                     reads=[X_b[k], cx.identf_b], writes=[cx.psb[bank]])

        def st5(it):
            j, k = it
            i = (j * 8 + k) % 2
            for hb in range(2):
                S.op("act", lambda e: e.copy(out=XTs[i][:, hb * 512:(hb + 1) * 512], in_=cx.ps[XT[hb]][:]),
                     reads=[cx.psb[XT[hb]]], writes=[XTs_b[i]])

        def st6(it):
            j, k = it
            i = (j * 8 + k) % 2
            for b in range(8):
                S.op("pe", lambda e: e.matmul(cx.ps[YT][:, 0:128], lhsT=CP[:, k, b, :], rhs=XTs[i][:, b * 128:(b + 1) * 128],
                                              start=(b == 0), stop=(b == 7)),
                     reads=[cp_b, XTs_b[i]], writes=[cx.psb[YT]])

        def st7(it):
            j, k = it
            i = (j * 8 + k) % 2
            J, tt = divmod(j, 4)
            u = uT[J % 2][:, k, tt * 128:(tt + 1) * 128]
            S.op("dve", lambda e: e.scalar_tensor_tensor(out=vt[i][:], in0=u, scalar=dsk[:, k:k + 1], in1=cx.ps[YT][:, 0:128],
                                                         op0=ALU.mult, op1=ALU.add),
                 reads=[uT_b[J % 2], cst_b, cx.psb[YT]], writes=[vt_b[i]])
            S.op("act", lambda e: e.activation(out=GTs[J % 2][:, k, tt * 128:(tt + 1) * 128], in_=vt[i][:], func=AF.Gelu),
                 reads=[vt_b[i]], writes=[GTs_b[J % 2]])
            if k == 7 and tt == 3:
                S.dma("sp", gT_out[J], GTs[J % 2][:], reads=[GTs_b[J % 2]])

        pipeline([st0, st1, st2, st3, st4, st5, st6, st7], items)
    S.barrier()


def proj_phase(cx, h_in, aT_in, w_ap, N, glu, g_ap, b_ap, h_out, hT_out, T, after_setup=None):
    nc, S = cx.nc, cx.S
    NT = T // 512
    with ExitStack() as es:
        Wt = SB(nc, es, ("Wp", [128, 8, N], BF16))
        w_b = Buf("Wp")
        wv = w_ap.rearrange("(c p) n -> p c n", p=128)
        for c in range(8):
            S.dma("pool", Wt[:, c, :], wv[:, c, :], writes=[w_b])
        if after_setup is not None:
            after_setup()
        gbt = ln_epilogue_setup(cx, es, g_ap, b_ap)
        epi = Epi(cx, es, gbt, DN_ALPHA, LN_EPS, h_in, h_out, hT_out, tr_banks=[6, 7], nb=4)
        aT = [SB(nc, es, ("paT%d" % i, [128, 8, 512], BF16)) for i in range(2)]
        aT_b = [Buf("aT0"), Buf("aT1")]
        sg = [SB(nc, es, ("psg%d" % i, [128, 1024], F32)) for i in range(2)]
        sg_b = [Buf("psg0"), Buf("psg1")]
        pending = []
        S.dma("sp", aT[0][:], aT_in[0], writes=[aT_b[0]])
        n = 0
        for J in range(NT):
            cur = J % 2
            if J + 1 < NT:
                S.dma("sp", aT[1 - cur][:], aT_in[J + 1], writes=[aT_b[1 - cur]])
            for tt in range(4):
                tok0 = J * 512 + tt * 128
                epi.prefetch_h(tok0)
                nb = N // 512
                pb = 0 if glu else 2 * ((J * 4 + tt) % 2)
                for nbk in range(nb):
                    for c in range(8):
                        S.op("pe", lambda e: e.matmul(cx.ps[pb + nbk][:], lhsT=aT[cur][:, c, tt * 128:(tt + 1) * 128],
                                                      rhs=Wt[:, c, nbk * 512:(nbk + 1) * 512], start=(c == 0), stop=(c == 7)),
                             reads=[aT_b[cur], w_b], writes=[cx.psb[pb + nbk]])
                flush(pending, keep=2)
                i = epi.k % epi.nb
                hbuf = epi.h[i]
                if glu:
                    k = n % 2
                    n += 1
                    for hf in range(2):
                        sl = slice(hf * 512, (hf + 1) * 512)
                        S.op("act", lambda e: e.activation(out=sg[k][:, sl], in_=cx.ps[2 + hf][:], func=AF.Sigmoid),
                             reads=[cx.psb[2 + hf]], writes=[sg_b[k]])
                        S.op("dve", lambda e: e.tensor_tensor(out=sg[k][:, sl], in0=cx.ps[hf][:], in1=sg[k][:, sl], op=ALU.mult),
                             reads=[cx.psb[hf]], writes=[sg_b[k]])
                    S.op("dve", lambda e: e.scalar_tensor_tensor(out=hbuf[:], in0=hbuf[:], scalar=epi.kres, in1=sg[k][:],
                                                                 op0=ALU.mult, op1=ALU.add),
                         reads=[sg_b[k]], writes=[epi.b["h"][i]])
                else:
                    for hf in range(2):
                        sl = slice(hf * 512, (hf + 1) * 512)
                        S.op("dve", lambda e: e.scalar_tensor_tensor(out=hbuf[:, sl], in0=hbuf[:, sl], scalar=epi.kres, in1=cx.ps[pb + hf][:],
                                                                     op0=ALU.mult, op1=ALU.add),
                             reads=[cx.psb[pb + hf]], writes=[epi.b["h"][i]])
                epi.run_post(tok0, pending)
        flush(pending)
    S.barrier()


CONV_W = 31
NEGV = -30000.0
CO_A, CO_Q, CO_KC, CO_VC, CO_KS, CO_VS, CO_KW, CO_VW, CO_G, CO_END = 0, 1024, 1536, 1664, 1792, 1920, 2048, 2176, 2304, 2328


def e1_phase(cx, hT_in, P, consts, SC, T, after_setup=None):
    nc, S = cx.nc, cx.S
    NT = T // 512
    with ExitStack() as es:
        Win = SB(nc, es, ("eWin", [128, 8, CO_END], BF16))
        win_b = Buf("eWin")
        wv = P["w_in"].rearrange("(c p) f -> p c f", p=128)
        wblocks = ((0, 256), (512, 768), (256, 512), (768, 1024), (1024, 1536), (1536, 2048), (2048, CO_END))
        wbufs = [Buf("eWin%d" % i) for i in range(len(wblocks))]
        for (c0_, c1_), wb_ in zip(wblocks, wbufs):
            S.dma("pool", Win[:, :, c0_:c1_], wv[:, :, c0_:c1_], writes=[wb_])

        def wb(col):
            for (c0_, c1_), wb_ in zip(wblocks, wbufs):
                if c0_ <= col < c1_:
                    return wb_
            raise ValueError(col)
        if after_setup is not None:
            after_setup()
        DG = SB(nc, es, ("DG", [128, 4 * CONV_W, 128], BF16))
        dg_b = Buf("DG")
        cw = SB(nc, es, ("cw", [128, 4, CONV_W], F32))
        cpar = SB(nc, es, ("cpar", [128, 3, 4], F32))
        onesf = SB(nc, es, ("onesf", [128, 128], F32))
        cst_b = Buf("e1cst")
        with nc.allow_non_contiguous_dma(reason="small parameter transposes"):
            for cc in range(4):
                S.dma("sp", cw[:, cc, :], P["conv_w"][:, cc * 128:(cc + 1) * 128].rearrange("k c -> c k"), writes=[cst_b],
                      allow_slow_non_contiguous=True)
            for i, nm in enumerate(("conv_b", "cln_g", "cln_b")):
                S.dma("sp", cpar[:, i, :], P[nm].rearrange("(cc c) -> c cc", c=128), writes=[cst_b], allow_slow_non_contiguous=True)
        S.op("dve", lambda e: e.memset(onesf[:], 1.0), writes=[cst_b])
        for cc in range(4):
            for k in range(CONV_W):
                S.op("dve", lambda e: e.tensor_scalar(out=DG[:, cc * CONV_W + k, :], in0=cx.identf[:], scalar1=cw[:, cc, k:k + 1],
                                                      scalar2=None, op0=ALU.mult),
                     reads=[cst_b, cx.identf_b], writes=[dg_b])
        hT = [SB(nc, es, ("e1hT%d" % i, [128, 8, 512], BF16)) for i in range(2)]
        hT_b = [Buf("e1hT0"), Buf("e1hT1")]
        aTb = [SB(nc, es, ("aTb%d" % i, [128, 4, 512 + 30], BF16)) for i in range(2)]
        aTb_b = [Buf("aTb0"), Buf("aTb1")]
        sg = [SB(nc, es, ("e1sg%d" % i, [128, 512], BF16)) for i in range(2)]
        sg_b = [Buf("e1sg0"), Buf("e1sg1")]
        xs = SB(nc, es, ("xs", [128, 4, 512], F32))
        xq = SB(nc, es, ("xq", [128, 4, 512], F32))
        xs_b, xq_b = Buf("xs"), Buf("xq")
        mean = SB(nc, es, ("cmean", [128, 512], F32))
        rstd = SB(nc, es, ("crstd", [128, 512], F32))
        mean_b, rstd_b = Buf("mean"), Buf("rstd")
        yt = [SB(nc, es, ("e1y%d" % i, [128, 512], F32)) for i in range(2)]
        yt_b = [Buf("y0"), Buf("y1")]
        cst = SB(nc, es, ("catst", [128, 4, 512], BF16))
        cst_sb = Buf("catst")
        qst = SB(nc, es, ("qst", [64, 8, 512], BF16))
        kst = SB(nc, es, ("kst", [64, 8, 512], BF16))
        qst_b, kst_b = Buf("qst"), Buf("kst")
        vst = SB(nc, es, ("vst", [128, 4, 256], BF16))
        gst = SB(nc, es, ("gst", [128, 4, 24], F32))
        vst_b, gst_b = Buf("vst"), Buf("gst")
        S.dma("sp", hT[0][:], hT_in[0], writes=[hT_b[0]])
        nps = [0]

        def bank2():
            nps[0] += 1
            return nps[0] % 2

        for J in range(NT):
            cur = J % 2
            if J + 1 < NT:
                S.dma("sp", hT[1 - cur][:], hT_in[J + 1], writes=[hT_b[1 - cur]])
            ab = aTb[cur]
            if J == 0:
                S.op("pool", lambda e: e.memset(ab[:, :, 0:30], 0.0), writes=[aTb_b[cur]])
            else:
                S.op("pool", lambda e: e.tensor_copy(out=ab[:, :, 0:30], in_=aTb[1 - cur][:, :, 512:542]),
                     reads=[aTb_b[1 - cur]], writes=[aTb_b[cur]])
            for cc in range(4):
                pv, pg = 0 + cc % 2, 2 + cc % 2
                for c in range(8):
                    S.op("pe", lambda e: e.matmul(cx.ps[pv][:], lhsT=Win[:, c, cc * 128:(cc + 1) * 128], rhs=hT[cur][:, c, :],
                                                  start=(c == 0), stop=(c == 7)), reads=[wb(cc * 128), hT_b[cur]], writes=[cx.psb[pv]])
                for c in range(8):
                    S.op("pe", lambda e: e.matmul(cx.ps[pg][:], lhsT=Win[:, c, 512 + cc * 128:512 + (cc + 1) * 128], rhs=hT[cur][:, c, :],
                                                  start=(c == 0), stop=(c == 7)), reads=[wb(512 + cc * 128), hT_b[cur]], writes=[cx.psb[pg]])
                k = cc % 2
                S.op("act", lambda e: e.activation(out=sg[k][:], in_=cx.ps[pg][:], func=AF.Sigmoid), reads=[cx.psb[pg]], writes=[sg_b[k]])
                S.op("dve", lambda e: e.tensor_tensor(out=ab[:, cc, 30:542], in0=cx.ps[pv][:], in1=sg[k][:], op=ALU.mult),
                     reads=[cx.psb[pv], sg_b[k]], writes=[aTb_b[cur]])
            chunks = [(CO_Q + i * 128, "q", i) for i in range(4)] + [(CO_KC, "k", 0), (CO_VC, "k", 1), (CO_KS, "k", 2), (CO_KW, "k", 3)]
            for col, kind, idx in chunks:
                bk = 4 + bank2()
                for c in range(8):
                    S.op("pe", lambda e: e.matmul(cx.ps[bk][:], lhsT=Win[:, c, col:col + 128], rhs=hT[cur][:, c, :],
                                                  start=(c == 0), stop=(c == 7)), reads=[wb(col), hT_b[cur]], writes=[cx.psb[bk]])
                if kind == "q":
                    S.op("act", lambda e: e.mul(out=qst[:, 2 * idx, :], in_=cx.ps[bk][0:64, :], mul=0.125),
                         reads=[cx.psb[bk]], writes=[qst_b])
                    S.op("dve", lambda e: e.tensor_scalar(out=qst[:, 2 * idx + 1, :], in0=cx.ps[bk][64:128, :], scalar1=0.125,
                                                          scalar2=None, op0=ALU.mult), reads=[cx.psb[bk]], writes=[qst_b])
                else:
                    S.op("act", lambda e: e.copy(out=kst[:, 2 * idx, :], in_=cx.ps[bk][0:64, :]), reads=[cx.psb[bk]], writes=[kst_b])
                    S.op("dve", lambda e: e.tensor_copy(out=kst[:, 2 * idx + 1, :], in_=cx.ps[bk][64:128, :]),
                         reads=[cx.psb[bk]], writes=[kst_b])
            S.dma("sp", SC.QA[:, 0:64, J * 512:(J + 1) * 512].rearrange("h p t -> p h t"), qst[:], reads=[qst_b])
            S.dma("sp", SC.KX[:, :, J * 512:(J + 1) * 512].rearrange("n p t -> p n t"), kst[:], reads=[kst_b])
            for sub in range(4):
                bk = 6 + sub % 2
                for c in range(8):
                    S.op("pe", lambda e: e.matmul(cx.ps[bk][:, 0:128], lhsT=hT[cur][:, c, sub * 128:(sub + 1) * 128],
                                                  rhs=Win[:, c, CO_VS:CO_VS + 128], start=(c == 0), stop=(c == 7)),
                         reads=[wb(CO_VS), hT_b[cur]], writes=[cx.psb[bk]])
                for c in range(8):
                    S.op("pe", lambda e: e.matmul(cx.ps[bk][:, 128:280], lhsT=hT[cur][:, c, sub * 128:(sub + 1) * 128],
                                                  rhs=Win[:, c, CO_VW:CO_END], start=(c == 0), stop=(c == 7)),
                         reads=[wb(CO_VW), hT_b[cur]], writes=[cx.psb[bk]])
                S.op("dve", lambda e: e.tensor_copy(out=vst[:, sub, :], in_=cx.ps[bk][:, 0:256]), reads=[cx.psb[bk]], writes=[vst_b])
                S.op("act", lambda e: e.activation(out=gst[:, sub, :], in_=cx.ps[bk][:, 256:280], func=AF.Sigmoid),
                     reads=[cx.psb[bk]], writes=[gst_b])
            S.dma("sp", SC.vtok[J * 512:(J + 1) * 512, :].rearrange("(s p) c -> p s c", p=128), vst[:], reads=[vst_b])
            S.dma("sp", SC.gates[J * 512:(J + 1) * 512, :].rearrange("(s p) c -> p s c", p=128), gst[:], reads=[gst_b])
            for cc in range(4):
                bk = 0 + cc % 2
                for k in range(CONV_W):
                    S.op("pe", lambda e: e.matmul(cx.ps[bk][:], lhsT=DG[:, cc * CONV_W + k, :], rhs=ab[:, cc, k:k + 512],
                                                  start=(k == 0), stop=(k == CONV_W - 1)), reads=[dg_b, aTb_b[cur]], writes=[cx.psb[bk]])
                S.op("act", lambda e: e.activation(out=xs[:, cc, :], in_=cx.ps[bk][:], func=AF.Identity, bias=cpar[:, 0, cc:cc + 1]),
                     reads=[cx.psb[bk], cst_b], writes=[xs_b])
                S.op("act", lambda e: e.activation(out=xq[:, cc, :], in_=xs[:, cc, :], func=AF.Square), reads=[xs_b], writes=[xq_b])
            for cc in range(4):
                S.op("pe", lambda e: e.matmul(cx.ps[2][:], lhsT=onesf[:], rhs=xs[:, cc, :], start=(cc == 0), stop=(cc == 3)),
                     reads=[cst_b, xs_b], writes=[cx.psb[2]])
            for cc in range(4):
                S.op("pe", lambda e: e.matmul(cx.ps[3][:], lhsT=onesf[:], rhs=xq[:, cc, :], start=(cc == 0), stop=(cc == 3)),
                     reads=[cst_b, xq_b], writes=[cx.psb[3]])
            S.op("dve", lambda e: e.tensor_scalar(out=mean[:], in0=cx.ps[2][:], scalar1=1.0 / 512, scalar2=None, op0=ALU.mult),
                 reads=[cx.psb[2]], writes=[mean_b])
            S.op("dve", lambda e: e.tensor_tensor(out=rstd[:], in0=mean[:], in1=mean[:], op=ALU.mult), reads=[mean_b], writes=[rstd_b])
            S.op("dve", lambda e: e.scalar_tensor_tensor(out=rstd[:], in0=cx.ps[3][:], scalar=1.0 / 512, in1=rstd[:],
                                                         op0=ALU.mult, op1=ALU.subtract), reads=[cx.psb[3]], writes=[rstd_b])
            S.op("dve", lambda e: e.tensor_scalar(out=rstd[:], in0=rstd[:], scalar1=LN_EPS, scalar2=None, op0=ALU.add),
                 writes=[rstd_b])
            S.op("act", lambda e: e.activation(out=rstd[:], in_=rstd[:], func=AF.Sqrt), writes=[rstd_b])
            S.op("dve", lambda e: e.reciprocal(out=rstd[:], in_=rstd[:]), writes=[rstd_b])
            for cc in range(4):
                k = cc % 2
                S.op("dve", lambda e: e.tensor_tensor(out=yt[k][:], in0=xs[:, cc, :], in1=mean[:], op=ALU.subtract),
                     reads=[xs_b, mean_b], writes=[yt_b[k]])
                S.op("dve", lambda e: e.tensor_tensor(out=yt[k][:], in0=yt[k][:], in1=rstd[:], op=ALU.mult),
                     reads=[rstd_b], writes=[yt_b[k]])
                S.op("act", lambda e: e.activation(out=cst[:, cc, :], in_=yt[k][:], func=AF.Silu, scale=cpar[:, 1, cc:cc + 1],
                                                   bias=cpar[:, 2, cc:cc + 1]), reads=[yt_b[k], cst_b], writes=[cst_sb])
            S.dma("sp", SC.catT[J][:, 0:4, :], cst[:], reads=[cst_sb])
    S.barrier()


def make_att_consts(T):
    import ml_dtypes
    bf = ml_dtypes.bfloat16
    pos = np.arange(T)
    qe = np.zeros((8, 4, T), np.float32)
    for hh in range(8):
        sl = 2.0 ** (-(hh + 1))
        qe[hh, 0] = sl * 128
        qe[hh, 1] = sl
        qe[hh, 2] = -sl * 128 * (pos // 128)
        qe[hh, 3] = -sl * (pos % 128)
    ke = np.stack([pos // 128, pos % 128, np.ones(T), np.ones(T)]).astype(np.float32)
    NCB = T // 16
    cend = np.arange(NCB) * 16 + 31
    kce = np.stack([cend // 128, cend % 128, np.ones(NCB), np.ones(NCB)]).astype(np.float32)
    nsel = T // 64
    eall = (pos[None, :] // 64 == np.arange(128)[:, None]).astype(np.float32)
    c = np.arange(NCB)
    ov = ((cend[:, None] >= np.arange(128)[None, :] * 64) & ((c * 16)[:, None] < (np.arange(128)[None, :] + 1) * 64)).astype(np.float32)
    nct = (NCB + 127) // 128
    ovp = np.zeros((nct * 128, 128), np.float32)
    ovp[:NCB] = ov
    ovp = ovp.reshape(nct, 128, 128).transpose(1, 0, 2)
    p = np.arange(128)[:, None]
    q = np.arange(512)[None, :]
    cmask = np.zeros((128, 5, 512), np.float32)
    for oi, o in enumerate((-128, -96, -64, -32, 0)):
        cmask[:, oi, :] = np.where(16 * (o + p) + 31 <= q, 0.0, NEGV)
    q1 = np.arange(128)[None, :]
    tri2 = np.zeros((128, 2, 128), np.float32)
    tri2[:, 0, :] = np.where(p > q1, NEGV, 0.0)
    tri2[:, 1, :] = np.where(p <= q1, NEGV, 0.0)
    m = np.arange(254)[None, :] - 126
    cur = (np.arange(128)[:, None] >= 64).astype(np.int64)
    forced = (m == cur) | (m == cur - 1)
    valid = m <= cur
    wm = np.where(valid & ~forced, 1.0, 0.0).astype(np.float32)
    wa = np.where(valid, np.where(forced, 1e4, 0.0), -1.0).astype(np.float32)
    return dict(c_qe=qe.astype(bf), c_ke=ke.astype(bf), c_kce=kce.astype(bf), c_eall=eall.astype(bf), c_ov=ovp.astype(bf),
                c_cmask=cmask.astype(bf), c_tri2=tri2.astype(bf), c_wm=wm, c_wa=wa)


def e3_phase(cx, P, CA, SC, T):
    nc, S = cx.nc, cx.S
    NQT = T // 512
    NKT = T // 128
    NCB = T // 16
    NCT = (NCB + 127) // 128
    with ExitStack() as es:
        KcA = SB(nc, es, ("KcA", [68, 2, NCT * 128], BF16))
        Vc = SB(nc, es, ("Vc", [128, NCT, 2, 193], BF16))
        kc_b, vc_b = Buf("KcA"), Buf("Vc")
        S.op("pool", lambda e: e.memset(KcA[:], 0.0), writes=[kc_b])
        S.op("pool", lambda e: e.memset(Vc[:], 0.0), writes=[vc_b])
        for g in range(2):
            S.dma("sp", KcA[64:68, g, 0:NCB], CA["c_kce"], writes=[kc_b])
            S.dma("sp", Vc[:, :, g, 65:193], CA["c_ov"], writes=[vc_b])
        S.op("pool", lambda e: e.memset(Vc[:, :, :, 64:65], 1.0), writes=[vc_b])
        with ExitStack() as ts:
            kcT = SB(nc, ts, ("kcT", [64, 4, T], BF16))
            kcT_b = Buf("kcT")
            S.dma("sp", kcT[:], SC.KX[0:4].rearrange("n p t -> p n t"), writes=[kcT_b])
            for kv, (w1n, w2n, pen) in enumerate((("w1_k", "w2_k", "pe_k"), ("w1_v", "w2_v", "pe_v"))):
                W1 = SB(nc, ts, ("W1_%d" % kv, [64, 32, 128], BF16))
                W2 = SB(nc, ts, ("W2_%d" % kv, [128, 64], BF16))
                peT = SB(nc, ts, ("peT%d" % kv, [64, 32], BF16))
                bias = SB(nc, ts, ("cb%d" % kv, [128, 1], F32))
                hid = SB(nc, ts, ("hid%d" % kv, [128, NCT * 128], BF16))
                wb, bb, hb = Buf("w1"), Buf("bias"), Buf("hid")
                S.dma("pool", W1[:], P[w1n].rearrange("(j d) h -> d j h", d=64), writes=[wb])
                S.dma("pool", W2[:], P[w2n], writes=[wb])
                with nc.allow_non_contiguous_dma(reason="tiny pe transpose"):
                    S.dma("pool", peT[:], P[pen].rearrange("j d -> d j"), writes=[wb], allow_slow_non_contiguous=True)
                S.op("pool", lambda e: e.memset(hid[:], 0.0), writes=[hb])
                for j in range(32):
                    S.op("pe", lambda e: e.matmul(cx.ps[7][:, 0:1], lhsT=W1[:, j, :], rhs=peT[:, j:j + 1], start=(j == 0), stop=(j == 31)),
                         reads=[wb], writes=[cx.psb[7]])
                S.op("dve", lambda e: e.tensor_copy(out=bias[:], in_=cx.ps[7][:, 0:1]), reads=[cx.psb[7]], writes=[bb])
                for g in range(2):
                    src = kcT[:, kv * 2 + g, :].rearrange("p (n s) -> p n s", s=16)
                    n0 = 0
                    while n0 < NCB - 1:
                        nn = min(512, NCB - 1 - n0)
                        bk = (n0 // 512) % 2
                        for j in range(32):
                            rhs = src[:, n0 + j // 16:n0 + j // 16 + nn, j % 16]
                            S.op("pe", lambda e: e.matmul(cx.ps[bk][:, 0:nn], lhsT=W1[:, j, :], rhs=rhs, start=(j == 0), stop=(j == 31)),
                                 reads=[wb, kcT_b], writes=[cx.psb[bk]])
                        S.op("act", lambda e: e.activation(out=hid[:, n0:n0 + nn], in_=cx.ps[bk][:, 0:nn], func=AF.Silu, bias=bias[:, 0:1]),
                             reads=[cx.psb[bk], bb], writes=[hb])
                        n0 += nn
                    if kv == 0:
                        for n0 in range(0, NCT * 128, 512):
                            nn = min(512, NCT * 128 - n0)
                            S.op("pe", lambda e: e.matmul(cx.ps[2][0:64, 0:nn], lhsT=W2[:], rhs=hid[:, n0:n0 + nn], start=True, stop=True),
                                 reads=[wb, hb], writes=[cx.psb[2]])
                            S.op("dve", lambda e: e.tensor_copy(out=KcA[0:64, g, n0:n0 + nn], in_=cx.ps[2][0:64, 0:nn]),
                                 reads=[cx.psb[2]], writes=[kc_b])
                    else:
                        for ct in range(NCT):
                            S.op("pe", lambda e: e.matmul(cx.ps[3][:, 0:64], lhsT=hid[:, ct * 128:(ct + 1) * 128], rhs=W2[:], start=True, stop=True),
                                 reads=[wb, hb], writes=[cx.psb[3]])
                            S.op("dve", lambda e: e.tensor_copy(out=Vc[:, ct, g, 0:64], in_=cx.ps[3][:, 0:64]),
                                 reads=[cx.psb[3]], writes=[vc_b])
            S.barrier()
        KAs = SB(nc, es, ("KAs", [68, 2, T], BF16))
        KAw = SB(nc, es, ("KAw", [68, 2, T], BF16))
        Vall = SB(nc, es, ("Vall", [128, NKT, 4, 65], BF16))
        EALL = SB(nc, es, ("EALL", [128, T], BF16))
        CM = SB(nc, es, ("CM", [128, 5, 512], BF16))
        TR2 = SB(nc, es, ("TR2", [128, 2, 128], BF16))
        WMA = SB(nc, es, ("WMA", [128, 2, 254], F32))
        kv_b, cst_b = Buf("kv"), Buf("attc")
        for g in range(2):
            S.dma("sp", KAs[0:64, g, :], SC.KX[4 + g], writes=[kv_b])
            S.dma("sp", KAw[0:64, g, :], SC.KX[6 + g], writes=[kv_b])
            S.dma("sp", KAs[64:68, g, :], CA["c_ke"], writes=[kv_b])
            S.dma("sp", KAw[64:68, g, :], CA["c_ke"], writes=[kv_b])
        for v in range(4):
            S.dma("sp", Vall[:, :, v, 0:64], SC.vtok[:, v * 64:(v + 1) * 64].rearrange("(kt p) d -> p kt d", p=128), writes=[kv_b])
        S.op("pool", lambda e: e.memset(Vall[:, :, :, 64:65], 1.0), writes=[kv_b])
        S.dma("sp", EALL[:], CA["c_eall"], writes=[cst_b])
        S.dma("sp", CM[:], CA["c_cmask"], writes=[cst_b])
        S.dma("sp", TR2[:], CA["c_tri2"], writes=[cst_b])
        S.dma("sp", WMA[:, 0, :], CA["c_wm"], writes=[cst_b])
        S.dma("sp", WMA[:, 1, :], CA["c_wa"], writes=[cst_b])
        QAs = [SB(nc, es, ("QAs%d" % i, [68, 4, 512], BF16)) for i in range(2)]
        QAs_b = [Buf("QA0"), Buf("QA1")]
        NPT = 6
        PT = [SB(nc, es, ("PT%d" % i, [128, 512], BF16)) for i in range(NPT)]
        PT_b = [Buf("PT%d" % i) for i in range(NPT)]
        NEGT = [SB(nc, es, ("NEGT%d" % i, [128, 512], BF16)) for i in range(2)]
        NEGT_b = [Buf("NEGT0"), Buf("NEGT1")]
        oacc = SB(nc, es, ("oacc", [128, 4, 512], F32))
        oacc_b = [Buf("oacc%d" % s) for s in range(4)]
        gat = [SB(nc, es, ("gat%d" % i, [128, 4, 24], F32)) for i in range(2)]
        gat_b = [Buf("gat0"), Buf("gat1")]
        imp = SB(nc, es, ("imp", [128, 4, 128], F32))
        imp_b = [Buf("imp%d" % s) for s in range(4)]
        sm = SB(nc, es, ("smalls", [128, 64], F32))
        sm_b = Buf("smalls")
        sc2 = SB(nc, es, ("sc2", [128, 128], F32))
        sc2_b = Buf("sc2")
        negb = SB(nc, es, ("negb", [128, 128], BF16))
        negb_b = Buf("negb")
        ob = SB(nc, es, ("ob", [128, 512], BF16))
        ob_b = Buf("ob")
        ost = SB(nc, es, ("ost", [128, 4, 512], BF16))
        ost_b = Buf("ost")
        ACC = (2, 3, 4, 5)
        cnt = {"st": 0, "pt": 0, "sm": 0, "mx": 0}
        MXs = [SB(nc, es, ("MXs%d" % i, [128, 512], BF16)) for i in range(3)]
        MXs_b = [Buf("MXs%d" % i) for i in range(3)]
        ZB = SB(nc, es, ("ZB", [128, 128], BF16))
        S.op("pool", lambda e: e.memset(ZB[:], 0.0), writes=[cst_b])

        STB = (0, 1, 2, 3)

        def acc_ap(banks, ncols, s_):
            spb = 4 if ncols <= 128 else 2
            bk = banks[s_ // spb]
            return cx.ps[bk][:, (s_ % spb) * ncols:(s_ % spb + 1) * ncols], bk

        def attend(qa, tiles, ncols, banks, filler=False):
            last = {}
            for i, t in enumerate(tiles):
                for s_ in range(t[1] // 128, t[2] // 128):
                    last[s_] = i
            for bk in banks:
                S.op("pe", lambda e: e.matmul(cx.ps[bk][:], lhsT=ZB[:], rhs=EALL[:, 0:512], start=True, stop=True, skip_group_check=True),
                     reads=[cst_b], writes=[cx.psb[bk]])
            pend = []
            for i, (Klhs, c0, c1, extras, V) in enumerate(tiles):
                sb = STB[cnt["st"] % len(STB)]
                cnt["st"] += 1
                pti = cnt["pt"] % NPT
                cnt["pt"] += 1
                S.op("pe", lambda e: e.matmul(cx.ps[sb][:, c0:c1], lhsT=Klhs, rhs=qa[:, c0:c1], start=True, stop=(len(extras) == 0)),
                     reads=[kv_b, kc_b, QAs_b[0], QAs_b[1]], writes=[cx.psb[sb]])
                for xi, (l2, r2, e0, e1) in enumerate(extras):
                    S.op("pe", lambda e: e.matmul(cx.ps[sb][:, e0:e1], lhsT=l2, rhs=r2, start=False, stop=(xi == len(extras) - 1)),
                         reads=[cst_b, cx.ident_b, NEGT_b[0], NEGT_b[1]], writes=[cx.psb[sb]])
                S.op("act", lambda e: e.activation(out=PT[pti][:, c0:c1], in_=cx.ps[sb][:, c0:c1], func=AF.Exp),
                     reads=[cx.psb[sb]], writes=[PT_b[pti]])
                if filler:
                    S.op("pe", lambda e: e.matmul(cx.ps[7][:], lhsT=ZB[:], rhs=EALL[:, 0:512], start=True, stop=True, skip_group_check=True),
                         reads=[cst_b], writes=[cx.psb[7]])

                def pv(i=i, pti=pti, c0=c0, c1=c1, V=V):
                    for s_ in range(c0 // 128, c1 // 128):
                        ap, bk = acc_ap(banks, ncols, s_)
                        S.op("pe", lambda e: e.matmul(ap, lhsT=PT[pti][:, s_ * 128:(s_ + 1) * 128], rhs=V, start=False, stop=(last[s_] == i),
                                                      skip_group_check=True),
                             reads=[PT_b[pti], kv_b, vc_b], writes=[cx.psb[bk]])
                pend.append(pv)
                flush(pend, keep=2)
            flush(pend)

        def finalize(g, j, br, gt, subs, first_branch, with_imp, banks):
            hh = g * 4 + j
            for s in subs:
                k = cnt["sm"] % 16
                cnt["sm"] += 1
                c = sm[:, 4 * k:4 * k + 4]
                acc, bk_ = acc_ap(banks, 193 if with_imp else 65, s)
                accb = cx.psb[bk_]
                S.op("dve", lambda e: e.tensor_scalar(out=c[:, 0:1], in0=acc[:, 64:65], scalar1=1e-30, scalar2=None, op0=ALU.max),
                     reads=[accb], writes=[sm_b])
                S.op("dve", lambda e: e.reciprocal(out=c[:, 1:2], in_=c[:, 0:1]), reads=[sm_b], writes=[sm_b])
                gcol = g * 12 + j * 3 + br
                S.op("dve", lambda e: e.tensor_tensor(out=c[:, 2:3], in0=c[:, 1:2], in1=gt[:, s, gcol:gcol + 1], op=ALU.mult),
                     reads=[sm_b, gat_b[0], gat_b[1]], writes=[sm_b])
                dst = oacc[:, s, hh * 64:(hh + 1) * 64]
                if first_branch:
                    S.op("dve", lambda e: e.tensor_scalar(out=dst, in0=acc[:, 0:64], scalar1=c[:, 2:3], scalar2=None, op0=ALU.mult),
                         reads=[accb, sm_b], writes=[oacc_b[s]])
                else:
                    S.op("dve", lambda e: e.scalar_tensor_tensor(out=dst, in0=acc[:, 0:64], scalar=c[:, 2:3], in1=dst, op0=ALU.mult, op1=ALU.add),
                         reads=[accb, sm_b], writes=[oacc_b[s]])
                if with_imp:
                    if j == 0:
                        S.op("dve", lambda e: e.tensor_scalar(out=imp[:, s, :], in0=acc[:, 65:193], scalar1=c[:, 1:2], scalar2=None, op0=ALU.mult),
                             reads=[accb, sm_b], writes=[imp_b[s]])
                    else:
                        S.op("dve", lambda e: e.scalar_tensor_tensor(out=imp[:, s, :], in0=acc[:, 65:193], scalar=c[:, 1:2], in1=imp[:, s, :],
                                                                     op0=ALU.mult, op1=ALU.add),
                             reads=[accb, sm_b], writes=[imp_b[s]])

        nq = 0
        for qt in range(NQT):
            gt = gat[qt % 2]
            S.dma("sp", gt[:], SC.gates[qt * 512:(qt + 1) * 512, :].rearrange("(s p) c -> p s c", p=128), writes=[gat_b[qt % 2]])
            for g in range(2):
                qi = nq % 2
                nq += 1
                qa4 = QAs[qi]
                S.dma("sp", qa4[:], SC.QA[g * 4:(g + 1) * 4, :, qt * 512:(qt + 1) * 512].rearrange("h p t -> p h t"), writes=[QAs_b[qi]])
                ctmax = min(NCT - 1, (32 * qt + 30) // 128)
                for j in range(4):
                    tiles = []
                    for ct in range(ctmax + 1):
                        o = 128 * ct - 32 * qt
                        extras = []
                        if o + 127 >= -1:
                            oi = (o + 128) // 32
                            extras.append((cx.ident[:], CM[:, oi, :], 0, 512))
                        tiles.append((KcA[:, g, ct * 128:(ct + 1) * 128], 0, 512, extras, Vc[:, ct, g, :]))
                    attend(qa4[:, j, :], tiles, 193, (4, 5), filler=True)
                    finalize(g, j, 0, gt, range(4), True, True, (4, 5))
                for s in range(4):
                    qs = 4 * qt + s
                    off = 126 - 2 * qs
                    S.op("dve", lambda e: e.tensor_tensor(out=sc2[:], in0=imp[:, s, :], in1=WMA[:, 0, off:off + 128], op=ALU.mult),
                         reads=[imp_b[s], cst_b], writes=[sc2_b])
                    S.op("dve", lambda e: e.tensor_tensor(out=sc2[:], in0=sc2[:], in1=WMA[:, 1, off:off + 128], op=ALU.add),
                         reads=[cst_b], writes=[sc2_b])
                    S.op("dve", lambda e: e.memset(sc2[:, 0:1], 1e4), writes=[sc2_b])
                    k = cnt["sm"] % 2
                    cnt["sm"] += 1
                    m8 = sm[:, 32 * k:32 * k + 8]
                    m8b = sm[:, 32 * k + 8:32 * k + 16]
                    S.op("dve", lambda e: e.max(out=m8, in_=sc2[:]), reads=[sc2_b], writes=[sm_b])
                    S.op("dve", lambda e: e.match_replace(out=imp[:, s, :], in_to_replace=m8, in_values=sc2[:], imm_value=-1e9),
                         reads=[sc2_b, sm_b], writes=[imp_b[s]])
                    S.op("dve", lambda e: e.max(out=m8b, in_=imp[:, s, :]), reads=[imp_b[s]], writes=[sm_b])
                    S.op("dve", lambda e: e.tensor_scalar(out=m8b[:, 7:8], in0=m8b[:, 7:8], scalar1=0.0, scalar2=None, op0=ALU.max),
                         reads=[sm_b], writes=[sm_b])
                    S.op("dve", lambda e: e.tensor_scalar(out=negb[:], in0=sc2[:], scalar1=m8b[:, 7:8], scalar2=NEGV, op0=ALU.is_lt, op1=ALU.mult),
                         reads=[sc2_b, sm_b], writes=[negb_b])
                    trv = cx.ps[6][:].bitcast(BF16)
                    S.op("pe", lambda e: e.transpose(out=trv[:, 0:128], in_=negb[:], identity=cx.ident[:]),
                         reads=[negb_b, cx.ident_b], writes=[cx.psb[6]])
                    S.op("act", lambda e: e.copy(out=NEGT[g][:, s * 128:(s + 1) * 128], in_=trv[:, 0:128]),
                         reads=[cx.psb[6]], writes=[NEGT_b[g]])
                for j in range(4):
                    tiles = []
                    for d in range(4):
                        kt = 4 * qt - 4 + d
                        if kt >= 0:
                            tiles.append((KAw[:, g, kt * 128:(kt + 1) * 128], 0, 128 * (d + 1),
                                          [(cx.ident[:], TR2[:, 1, :], 128 * d, 128 * d + 128)], Vall[:, kt, 2 + g, :]))
                    for d in range(4):
                        kt = 4 * qt + d
                        tiles.append((KAw[:, g, kt * 128:(kt + 1) * 128], 128 * d, 512,
                                      [(cx.ident[:], TR2[:, 0, :], 128 * d, 128 * d + 128)], Vall[:, kt, 2 + g, :]))
                    bks = (4 + j % 2,)
                    attend(qa4[:, j, :], tiles, 65, bks, filler=True)
                    finalize(g, j, 2, gt, range(4), False, False, bks)
                for j in range(4):
                    tiles = []
                    for kt in range(4 * qt + 4):
                        d = kt - 4 * qt
                        c0 = 0 if d < 0 else 128 * d
                        extras = [(EALL[:, kt * 128:(kt + 1) * 128], NEGT[g][:, c0:512], c0, 512)]
                        if d >= 0:
                            extras.append((cx.ident[:], TR2[:, 0, :], c0, c0 + 128))
                        tiles.append((KAs[:, g, kt * 128:(kt + 1) * 128], c0, 512, extras, Vall[:, kt, g, :]))
                    bks = (4 + j % 2,)
                    attend(qa4[:, j, :], tiles, 65, bks)
                    finalize(g, j, 1, gt, range(4), False, False, bks)
            for s in range(4):
                S.op("act", lambda e: e.copy(out=ob[:], in_=oacc[:, s, :]), reads=[oacc_b[s]], writes=[ob_b])
                trv = cx.ps[7][:].bitcast(BF16)
                for f in range(4):
                    S.op("pe", lambda e: e.transpose(out=trv[:, f * 128:(f + 1) * 128], in_=ob[:, f * 128:(f + 1) * 128], identity=cx.ident[:]),
                         reads=[ob_b, cx.ident_b], writes=[cx.psb[7]])
                S.op("dve", lambda e: e.tensor_copy(out=ost[:, :, s * 128:(s + 1) * 128], in_=trv[:, 0:512].rearrange("p (f t) -> p f t", f=4)),
                     reads=[cx.psb[7]], writes=[ost_b])
            S.dma("sp", SC.catT[qt][:, 4:8, :], ost[:], reads=[ost_b])
    S.barrier()


def xpose_phase(cx, h_in, hT_out, T):
    nc, S = cx.nc, cx.S
    with ExitStack() as es:
        ht = [SB(nc, es, ("xp_h%d" % i, [128, D], F32)) for i in range(2)]
        hb = [SB(nc, es, ("xp_hb%d" % i, [128, D], BF16)) for i in range(2)]
        st = [SB(nc, es, ("xp_st%d" % i, [128, 8, 512], BF16)) for i in range(2)]
        ht_b, hb_b, st_b = [Buf(), Buf()], [Buf(), Buf()], [Buf(), Buf()]
        for n in range(T // 128):
            i = n % 2
            J, tt = divmod(n, 4)
            S.dma("sp", ht[i][:], h_in[n * 128:(n + 1) * 128, :], writes=[ht_b[i]])
            S.op("dve", lambda e: e.tensor_copy(out=hb[i][:], in_=ht[i][:]), reads=[ht_b[i]], writes=[hb_b[i]])
            bank = 6 + i
            trv = cx.ps[bank][:].bitcast(BF16)
            for c in range(8):
                S.op("pe", lambda e: e.transpose(out=trv[:, c * 128:(c + 1) * 128], in_=hb[i][:, c * 128:(c + 1) * 128], identity=cx.ident[:]),
                     reads=[hb_b[i], cx.ident_b], writes=[cx.psb[bank]])
            S.op("act", lambda e: e.copy(out=st[J % 2][:, :, tt * 128:(tt + 1) * 128], in_=trv.rearrange("p (c t) -> p c t", c=8)),
                 reads=[cx.psb[bank]], writes=[st_b[J % 2]])
            if tt == 3:
                S.dma("sp", hT_out[J], st[J % 2][:], reads=[st_b[J % 2]])
    S.barrier()


EV_NAMES = ["w_in", "conv_w", "conv_b", "cln_g", "cln_b", "pe_k", "w1_k", "w2_k", "pe_v", "w1_v", "w2_v", "w_out"]
OD_NAMES = ["a_re", "a_im", "log_dt", "b_re", "b_im", "c_re", "c_im", "d", "w_glu"]
CA_NAMES = ["c_qe", "c_ke", "c_kce", "c_eall", "c_ov", "c_cmask", "c_tri2", "c_wm", "c_wa"]


def build_program(T, layers=range(DEPTH)):
    nc = bass.Bass("TRN2", target_bir_lowering=False)
    dt = lambda n, sh, ty=F32, kind="ExternalInput": nc.dram_tensor(n, list(sh), ty, kind=kind).ap()
    x = dt("x", [T, D])
    out = dt("out", [T, D], F32, "ExternalOutput")
    W = {}
    W["ffn1_w_in"] = dt("ffn1_w_in", [DEPTH, D, 2 * FF])
    W["ffn1_w_out"] = dt("ffn1_w_out", [DEPTH, FF, D])
    W["ffn2_w_in"] = dt("ffn2_w_in", [DEPTH, D, 2 * FF])
    W["ffn2_w_out"] = dt("ffn2_w_out", [DEPTH, FF, D])
    W["ln_g"] = dt("ln_g", [DEPTH, 3, D])
    W["ln_b"] = dt("ln_b", [DEPTH, 3, D])
    ev_shapes = dict(w_in=[2, D, CO_END], conv_w=[2, 31, 512], conv_b=[2, 512], cln_g=[2, 512], cln_b=[2, 512], pe_k=[2, 32, 64],
                     w1_k=[2, 2048, 128], w2_k=[2, 128, 64], pe_v=[2, 32, 64], w1_v=[2, 2048, 128], w2_v=[2, 128, 64], w_out=[2, D, D])
    od_shapes = dict(a_re=[2, 64, 64], a_im=[2, 64, 64], log_dt=[2, 64], b_re=[2, 64, 64, 16], b_im=[2, 64, 64, 16],
                     c_re=[2, 64, 16, 64], c_im=[2, 64, 16, 64], d=[2, D], w_glu=[2, D, 2 * D])
    EV = {n: dt("ev_" + n, ev_shapes[n]) for n in EV_NAMES}
    OD = {n: dt("od_" + n, od_shapes[n]) for n in OD_NAMES}
    consts = dt("consts", [128, C_END])
    NCB = T // 16
    NCT = (NCB + 127) // 128
    ca_shapes = dict(c_qe=([8, 4, T], BF16), c_ke=([4, T], BF16), c_kce=([4, NCB], BF16), c_eall=([128, T], BF16),
                     c_ov=([128, NCT, 128], BF16), c_cmask=([128, 5, 512], BF16), c_tri2=([128, 2, 128], BF16),
                     c_wm=([128, 254], F32), c_wa=([128, 254], F32))
    CA = {n: dt(n, ca_shapes[n][0], ca_shapes[n][1]) for n in CA_NAMES}
    it = lambda n, sh, ty=F32: nc.dram_tensor(n, list(sh), ty, kind="Internal").ap()
    hbuf = [it("h_s%d" % i, [T, D]) for i in range(2)]
    hTbuf = [it("hT_s%d" % i, [T // 512, 128, 8, 512], BF16) for i in range(2)]
    SC = Ctx()
    SC.catT = it("catT", [T // 512, 128, 8, 512], BF16)
    SC.QA = it("QA", [8, 68, T], BF16)
    SC.KX = it("KX", [8, 64, T], BF16)
    SC.vtok = it("vtok", [T, 256], BF16)
    SC.gates = it("gates", [T, 24], F32)
    with ExitStack() as es:
        cx = setup_ctx(nc, es, consts[:, 0:128])
        S = cx.S
        S.dma("sp", SC.QA[:, 64:68, :], CA["c_qe"])
        wscr = [(it("wsin%d" % i, [D, 2 * FF], BF16), it("wsout%d" % i, [FF, D], BF16)) for i in range(2)]
        layers = list(layers)
        cast_ffn_weights(cx, W["ffn1_w_in"][layers[0]], W["ffn1_w_out"][layers[0]], wscr[0])
        xpose_phase(cx, x, hTbuf[0], T)
        h_cur, hT_cur, pp = x, hTbuf[0], 0

        def nxt():
            nonlocal pp
            pp ^= 1
            return hbuf[pp], hTbuf[pp]

        for li, layer in enumerate(layers):
            i = layer // 2
            ho, hTo = nxt()
            ffn_phase(cx, h_cur, hT_cur, wscr[0], W["ln_g"][layer, 0], W["ln_b"][layer, 0], ho, hTo, T)
            pre2 = lambda layer=layer: cast_ffn_weights(cx, W["ffn2_w_in"][layer], W["ffn2_w_out"][layer], wscr[1])
            if li + 1 < len(layers):
                nl = layers[li + 1]
                pre1 = lambda nl=nl: cast_ffn_weights(cx, W["ffn1_w_in"][nl], W["ffn1_w_out"][nl], wscr[0])
            else:
                pre1 = None
            h_cur, hT_cur = ho, hTo
            ho, hTo = nxt()
            if layer % 2 == 0:
                P = {n: EV[n][i] for n in EV_NAMES}
                e1_phase(cx, hT_cur, P, consts, SC, T, after_setup=pre2)
                e3_phase(cx, P, CA, SC, T)
                proj_phase(cx, h_cur, SC.catT, P["w_out"], D, False, W["ln_g"][layer, 1], W["ln_b"][layer, 1], ho, hTo, T, after_setup=pre1)
            else:
                P = {n: OD[n][i] for n in OD_NAMES}
                s5_phase(cx, hT_cur, P, consts, SC.catT, T, after_setup=pre2)
                proj_phase(cx, h_cur, SC.catT, P["w_glu"], 2 * D, True, W["ln_g"][layer, 1], W["ln_b"][layer, 1], ho, hTo, T, after_setup=pre1)
            h_cur, hT_cur = ho, hTo
            last = (li == len(layers) - 1)
            if last:
                ho, hTo = out, None
            else:
                ho, hTo = nxt()
            ffn_phase(cx, h_cur, hT_cur, wscr[1], W["ln_g"][layer, 2], W["ln_b"][layer, 2], ho, hTo, T)
            h_cur, hT_cur = ho, hTo
        S.final_wait("sp")
        nc._mk_stats = (S.n_inst, S.n_wait)
    return nc


_PROG = {}


def kernel(**inputs):
    x = np.asarray(inputs["x"], np.float32)
    B, T, _ = x.shape
    if T not in _PROG:
        _PROG[T] = build_program(T)
    nc = _PROG[T]
    shared = {k: np.ascontiguousarray(np.asarray(v, np.float32)) for k, v in inputs.items() if k != "x"}
    shared["consts"] = make_consts()
    shared.update(make_att_consts(T))
    workers = [0, 1, 4, 5][:B] if B <= 4 else list(range(B))
    MIRROR = False
    big = ["ffn1_w_in", "ffn1_w_out", "ffn2_w_in", "ffn2_w_out", "ev_w_in", "ev_w_out", "od_w_glu", "ev_w1_k", "ev_w1_v"]
    idle = dict(shared)
    for k in big:
        idle[k] = np.zeros_like(shared[k])
    idle["x"] = np.zeros((T, D), np.float32)
    in_maps = []
    for core in range(8):
        if core in workers:
            m = dict(shared)
            m["x"] = np.ascontiguousarray(x[workers.index(core)])
        elif MIRROR:
            m = dict(shared)
            m["x"] = np.ascontiguousarray(x[[2, 3, 6, 7].index(core) % B])
        else:
            m = idle
        in_maps.append(m)
    res = run_bass_kernel_spmd(nc, in_maps, core_ids=list(range(8)))
    return np.stack([np.asarray(res.results[workers[b]]["out"], np.float32) for b in range(B)], axis=0)
```

```python
import math
from contextlib import ExitStack
import numpy as np
import concourse.bass as bass
import concourse.mybir as mybir
from concourse.bass_utils import run_bass_kernel_spmd

F32 = mybir.dt.float32
BF16 = mybir.dt.bfloat16
AF = mybir.ActivationFunctionType
ALU = mybir.AluOpType

D = 1024
FF = 2816
DEPTH = 4
DN_ALPHA = (2.0 * DEPTH) ** 0.25
LN_EPS = 1e-5


_NM = [0]


def SB(nc, es, args):
    name, shape, dt = args
    _NM[0] += 1
    return es.enter_context(nc.sbuf_tensor("%s_%d" % (name, _NM[0]), shape, dt))


class Buf:
    __slots__ = ("name", "w", "r")

    def __init__(self, name=""):
        self.name = name
        self.w = None
        self.r = {}


class Sched:
    ROT = 30000

    def __init__(self, nc, es, n_dma=24):
        self.nc = nc
        self.es = es
        self.engs = {"pe": nc.tensor, "act": nc.scalar, "dve": nc.vector, "pool": nc.gpsimd, "sp": nc.sync}
        self.cur = {}
        self.cnt = {}
        self.nsem = 0
        self.seen = {e: {} for e in self.engs}
        for e in self.engs:
            self._new_sem(e)
        self.dma_sems = [es.enter_context(nc.semaphore("dq%d" % i)) for i in range(n_dma)]
        self.dma_val = [0] * n_dma
        self.dma_slots = {"sp": list(range(0, n_dma - 8)), "pool": list(range(n_dma - 8, n_dma))}
        self.dma_next = {"sp": 0, "pool": 0}
        self.n_inst = 0
        self.n_wait = 0

    def _new_sem(self, e):
        self.cur[e] = self.es.enter_context(self.nc.semaphore("s_%s_%d" % (e, self.nsem)))
        self.nsem += 1
        self.cnt[e] = 0

    def _wait(self, e, ev):
        sem, val = ev
        k = id(sem)
        if self.seen[e].get(k, 0) >= val:
            return
        self.engs[e].wait_ge(sem, val)
        self.seen[e][k] = val
        self.n_wait += 1

    def _deps(self, e, reads, writes):
        own = id(self.cur[e])
        for b in reads:
            if b.w is not None:
                if e == "pe" and id(b.w[0]) == own:
                    continue
                self._wait(e, b.w)
        for b in writes:
            if b.w is not None and not (e == "pe" and id(b.w[0]) == own):
                self._wait(e, b.w)
            for ev in b.r.values():
                if e == "pe" and id(ev[0]) == own:
                    continue
                self._wait(e, ev)

    def _mark(self, ev, reads, writes):
        k = id(ev[0])
        for b in reads:
            b.r[k] = ev
        for b in writes:
            b.w = ev
            b.r = {}

    def op(self, e, fn, reads=(), writes=()):
        self._deps(e, reads, writes)
        if self.cnt[e] >= self.ROT:
            self._new_sem(e)
        inst = fn(self.engs[e])
        self.cnt[e] += 1
        inst.then_inc(self.cur[e], 1)
        ev = (self.cur[e], self.cnt[e])
        self._mark(ev, reads, writes)
        self.n_inst += 1
        return ev

    def dma(self, q, out, in_, reads=(), writes=(), **kw):
        self._deps(q, reads, writes)
        slots = self.dma_slots[q]
        i = slots[self.dma_next[q] % len(slots)]
        self.dma_next[q] += 1
        sem = self.dma_sems[i]
        if self.dma_val[i] > 0:
            self._wait(q, (sem, self.dma_val[i]))
        self.engs[q].dma_start(out=out, in_=in_, **kw).then_inc(sem, 16)
        self.dma_val[i] += 16
        ev = (sem, self.dma_val[i])
        self._mark(ev, reads, writes)
        self.n_inst += 1
        return ev

    def barrier(self):
        evs = [(self.cur[e], self.cnt[e]) for e in self.engs if self.cnt[e] > 0]
        evs += [(s, v) for s, v in zip(self.dma_sems, self.dma_val) if v > 0]
        for e in self.engs:
            for ev in evs:
                if ev[0] is self.cur[e]:
                    continue
                self._wait(e, ev)

    def final_wait(self, e="sp"):
        for s, v in zip(self.dma_sems, self.dma_val):
            if v > 0:
                self._wait(e, (s, v))


class Ctx:
    pass


def setup_ctx(nc, es, ident_ap):
    cx = Ctx()
    cx.nc = nc
    cx.S = Sched(nc, es)
    S = cx.S
    cx.ps = [es.enter_context(nc.psum_tensor("ps%d" % i, [128, 512], F32)) for i in range(8)]
    cx.psb = [Buf("ps%d" % i) for i in range(8)]
    cx.ident = SB(nc, es, ("ident_sb", [128, 128], BF16))
    cx.ident_b = Buf("ident")
    S.dma("pool", cx.ident[:], ident_ap, writes=[cx.ident_b])
    cx.identf = SB(nc, es, ("identf", [128, 128], F32))
    cx.identf_b = Buf("identf")
    S.dma("sp", cx.identf[:], ident_ap, writes=[cx.identf_b])
    return cx


def ln_epilogue_setup(cx, es, g_ap, b_ap):
    nc, S = cx.nc, cx.S
    G = SB(nc, es, ("lnG", [128, D], F32))
    B = SB(nc, es, ("lnB", [128, D], F32))
    gb = Buf("lnG")
    bb = Buf("lnB")
    S.dma("sp", G[:], g_ap.partition_broadcast(128), writes=[gb])
    S.dma("sp", B[:], b_ap.partition_broadcast(128), writes=[bb])
    return (G, gb, B, bb)


class Epi:
    def __init__(self, cx, es, gbt, kres, eps, h_in, h_out, hT_out, tr_banks, nb=2):
        nc = cx.nc
        self.cx = cx
        self.G, self.gb, self.B, self.bb = gbt
        self.kres = kres
        self.eps = eps
        self.h_in, self.h_out, self.hT_out = h_in, h_out, hT_out
        self.tr_banks = tr_banks
        self.nb = nb
        mk = lambda n, sh, dt: [SB(nc, es, ("%s%d" % (n, i), sh, dt)) for i in range(self.nb)]
        self.h = mk("ep_h", [128, D], F32)
        self.r = self.h
        self.xn = self.h
        self.ho = self.h
        self.hb = mk("ep_hb", [128, D], BF16)
        self.st = mk("ep_st", [128, 16], F32)
        self.stage = [SB(nc, es, ("ep_stage%d" % i, [128, 8, 512], BF16)) for i in range(1)]
        self.b = {n: [Buf(n + str(i)) for i in range(self.nb)] for n in ("h", "hb", "st")}
        for n in ("r", "xn", "ho"):
            self.b[n] = self.b["h"]
        self.stage_b = [Buf("stage%d" % i) for i in range(1)]
        self.k = 0

    def prefetch_n(self, n):
        cx = self.cx
        i = n % self.nb
        cx.S.dma("sp", self.h[i][:], self.h_in[n * 128:(n + 1) * 128, :], writes=[self.b["h"][i]])

    def run(self, tok0, y_ap, y_bufs, pending):
        cx = self.cx
        S = cx.S
        i = self.k % self.nb
        self.k += 1
        b = self.b
        h, r, xn, ho, hb, st = self.h[i], self.r[i], self.xn[i], self.ho[i], self.hb[i], self.st[i]
        S.op("dve", lambda e: e.scalar_tensor_tensor(out=r[:], in0=h[:], scalar=self.kres, in1=y_ap,
                                                     op0=ALU.mult, op1=ALU.add),
             reads=[b["h"][i]] + list(y_bufs), writes=[b["r"][i]])
        self.k -= 1
        self.run_post(tok0, pending)

    def run_post(self, tok0, pending):
        cx = self.cx
        S = cx.S
        i = self.k % self.nb
        self.k += 1
        b = self.b
        h, r, xn, ho, hb, st = self.h[i], self.r[i], self.xn[i], self.ho[i], self.hb[i], self.st[i]
        S.op("dve", lambda e: e.bn_stats(out=st[:, 0:6], in_=r[:, 0:512]), reads=[b["r"][i]], writes=[b["st"][i]])
        S.op("dve", lambda e: e.bn_stats(out=st[:, 6:12], in_=r[:, 512:1024]), reads=[b["r"][i]], writes=[b["st"][i]])
        S.op("dve", lambda e: e.bn_aggr(out=st[:, 12:14], in_=st[:, 0:12]), reads=[b["st"][i]], writes=[b["st"][i]])
        S.op("dve", lambda e: e.tensor_scalar(out=st[:, 14:15], in0=st[:, 13:14], scalar1=self.eps, scalar2=None,
                                              op0=ALU.add), reads=[b["st"][i]], writes=[b["st"][i]])
        S.op("act", lambda e: e.activation(out=st[:, 14:15], in_=st[:, 14:15], func=AF.Sqrt),
             reads=[b["st"][i]], writes=[b["st"][i]])
        S.op("dve", lambda e: e.reciprocal(out=st[:, 14:15], in_=st[:, 14:15]), reads=[b["st"][i]], writes=[b["st"][i]])
        S.op("dve", lambda e: e.tensor_scalar(out=st[:, 15:16], in0=st[:, 12:13], scalar1=-1.0, scalar2=st[:, 14:15],
                                              op0=ALU.mult, op1=ALU.mult), reads=[b["st"][i]], writes=[b["st"][i]])
        S.op("act", lambda e: e.activation(out=xn[:], in_=r[:], func=AF.Identity, scale=st[:, 14:15], bias=st[:, 15:16]),
             reads=[b["r"][i], b["st"][i]], writes=[b["xn"][i]])
        S.op("dve", lambda e: e.tensor_tensor(out=xn[:], in0=xn[:], in1=self.G[:], op=ALU.mult),
             reads=[self.gb], writes=[b["xn"][i]])
        S.op("dve", lambda e: e.tensor_tensor(out=ho[:], in0=xn[:], in1=self.B[:], op=ALU.add),
             reads=[b["xn"][i], self.bb], writes=[b["ho"][i]])
        S.dma("sp", self.h_out[tok0:tok0 + 128, :], ho[:], reads=[b["ho"][i]])
        if self.hT_out is None:
            return
        S.op("act", lambda e: e.copy(out=hb[:], in_=ho[:]), reads=[b["ho"][i]], writes=[b["hb"][i]])
        J, tt = divmod(tok0 // 128, 4)
        sidx = 0
        stage = self.stage[sidx]
        bank = self.tr_banks[(self.k - 1) % len(self.tr_banks)]

        def pe_work():
            trv = cx.ps[bank][:].bitcast(BF16)
            for c in range(8):
                S.op("pe", lambda e, c=c: e.transpose(out=trv[:, c * 128:(c + 1) * 128], in_=hb[:, c * 128:(c + 1) * 128],
                                                      identity=cx.ident[:]),
                     reads=[b["hb"][i], cx.ident_b], writes=[cx.psb[bank]])
            S.op("act", lambda e: e.copy(out=stage[:, :, tt * 128:(tt + 1) * 128],
                                         in_=trv.rearrange("p (c t) -> p c t", c=8)),
                 reads=[cx.psb[bank]], writes=[self.stage_b[sidx]])
            if tt == 3:
                S.dma("sp", self.hT_out[J], stage[:], reads=[self.stage_b[sidx]])

        pending.append(pe_work)


def flush(pending, keep=0):
    while len(pending) > keep:
        pending.pop(0)()


def cast_ffn_weights(cx, w_in, w_out, scr):
    S = cx.S
    S.dma("pool", scr[0].rearrange("r (a b) -> (r a) b", b=1408), w_in.rearrange("r (a b) -> (r a) b", b=1408))
    S.dma("pool", scr[1], w_out)


def ffn_phase(cx, h_in, hT_in, wscr, g_ap, b_ap, h_out, hT_out, T):
    nc, S = cx.nc, cx.S
    NT = T // 512
    with ExitStack() as es:
        Win = SB(nc, es, ("Win", [128, 8, 2 * FF], BF16))
        Wout = SB(nc, es, ("Wout", [128, 22, D], BF16))
        win_b = [Buf("Win%d" % c) for c in range(11)]
        wout_b = Buf("Wout")
        w_in_v = wscr[0].rearrange("(c p) f -> p c f", p=128)
        for blk in range(11):
            for off in (0, FF):
                S.dma("sp" if blk % 2 == 0 else "pool", Win[:, :, off + blk * 256:off + (blk + 1) * 256],
                      w_in_v[:, :, off + blk * 256:off + (blk + 1) * 256], writes=[win_b[blk]])
        w_out_v = wscr[1].rearrange("(c p) d -> p c d", p=128)
        for c0 in range(0, 22, 2):
            S.dma("pool", Wout[:, c0:c0 + 2, :], w_out_v[:, c0:c0 + 2, :], writes=[wout_b])
        gbt = ln_epilogue_setup(cx, es, g_ap, b_ap)
        epi = Epi(cx, es, gbt, DN_ALPHA / 0.5, LN_EPS / 0.25, h_in, h_out, hT_out, tr_banks=[6, 7])
        hT = [SB(nc, es, ("hTsb%d" % i, [128, 8, 512], BF16)) for i in range(2)]
        hT_b = [Buf("hT%d" % i) for i in range(2)]
        actT = SB(nc, es, ("actT", [128, 22, 512], BF16))
        act_b = [Buf("actT%d" % i) for i in range(22)]
        sg = [SB(nc, es, ("sg%d" % i, [128, 512], BF16)) for i in range(2)]
        sg_b = [Buf("sg%d" % i) for i in range(2)]
        pending = []
        S.dma("sp", hT[0][:], hT_in[0], writes=[hT_b[0]])
        epi.prefetch_n(0)
        for J in range(NT):
            cur = J % 2
            if J + 1 < NT:
                S.dma("sp", hT[1 - cur][:], hT_in[J + 1], writes=[hT_b[1 - cur]])
            for fc in range(22):
                pg, pv = fc % 2, 2 + fc % 2
                for c in range(8):
                    S.op("pe", lambda e, c=c: e.matmul(cx.ps[pg][:], lhsT=Win[:, c, fc * 128:(fc + 1) * 128],
                                                       rhs=hT[cur][:, c, :], start=(c == 0), stop=(c == 7)),
                         reads=[win_b[fc // 2], hT_b[cur]], writes=[cx.psb[pg]])
                for c in range(8):
                    S.op("pe", lambda e, c=c: e.matmul(cx.ps[pv][:], lhsT=Win[:, c, FF + fc * 128:FF + (fc + 1) * 128],
                                                       rhs=hT[cur][:, c, :], start=(c == 0), stop=(c == 7)),
                         reads=[win_b[fc // 2], hT_b[cur]], writes=[cx.psb[pv]])
                k = fc % 2
                S.op("act", lambda e: e.activation(out=sg[k][:], in_=cx.ps[pg][:], func=AF.Silu),
                     reads=[cx.psb[pg]], writes=[sg_b[k]])
                S.op("dve", lambda e: e.tensor_tensor(out=actT[:, fc, :], in0=cx.ps[pv][:], in1=sg[k][:], op=ALU.mult),
                     reads=[cx.psb[pv], sg_b[k]], writes=[act_b[fc]])
                if fc == 3:
                    flush(pending)
            for tt in range(4):
                tok0 = J * 512 + tt * 128
                if J * 4 + tt + 1 < NT * 4:
                    epi.prefetch_n(J * 4 + tt + 1)
                for half in range(2):
                    bank = 4 + half
                    for fc in range(22):
                        S.op("pe", lambda e, fc=fc: e.matmul(cx.ps[bank][:], lhsT=actT[:, fc, tt * 128:(tt + 1) * 128],
                                                             rhs=Wout[:, fc, half * 512:(half + 1) * 512],
                                                             start=(fc == 0), stop=(fc == 21)),
                             reads=[act_b[fc], wout_b], writes=[cx.psb[bank]])
                flush(pending)
                _epi_from_banks(cx, epi, tok0, pending)
        flush(pending)
    S.barrier()


def _epi_from_banks(cx, epi, tok0, pending):
    S = cx.S
    i = epi.k % epi.nb
    r, h = epi.r[i], epi.h[i]
    for half in range(2):
        bank = 4 + half
        sl = slice(half * 512, (half + 1) * 512)
        S.op("dve", lambda e: e.scalar_tensor_tensor(out=r[:, sl], in0=h[:, sl], scalar=epi.kres, in1=cx.ps[bank][:],
                                                     op0=ALU.mult, op1=ALU.add),
             reads=[epi.b["h"][i], cx.psb[bank]], writes=[epi.b["r"][i]])
    epi.run_post(tok0, pending)


C_IDENT, C_TRI, C_SEL, C_MASK, C_SP1, C_NSP1, C_END = 0, 128, 256, 384, 896, 897, 898
TWO_PI = 2.0 * math.pi


def make_consts():
    c = np.zeros((128, C_END), np.float32)
    c[:, C_IDENT:C_IDENT + 128] = np.eye(128)
    s = np.arange(128)
    c[:, C_TRI:C_TRI + 128] = (s[:, None] <= s[None, :])
    c[127, C_SEL:C_SEL + 128] = 1.0
    for j in range(4):
        m = np.zeros((128, 128), np.float32)
        for g2 in range(2):
            gl = 2 * j + g2
            m[g2 * 64:(g2 + 1) * 64, gl * 16:(gl + 1) * 16] = 1.0
        c[:, C_MASK + j * 128:C_MASK + (j + 1) * 128] = m
    c[:, C_SP1] = s + 1
    c[:, C_NSP1] = -(s + 1.0)
    return c


def pipeline(stages, items):
    n, m = len(items), len(stages)
    for tau in range(n + m - 1):
        for s in reversed(range(m)):
            i = tau - s
            if 0 <= i < n:
                stages[s](items[i])


def sincos(cx, es, theta, tb, F, sin_out, cos_out, ob):
    nc, S = cx.nc, cx.S
    q = SB(nc, es, ("sc_q", [128, F], F32))
    qi = SB(nc, es, ("sc_qi", [128, F], mybir.dt.int32))
    qb, qib = Buf("q"), Buf("qi")
    for shift, out in ((0.0, sin_out), (0.5 * math.pi, cos_out)):
        S.op("dve", lambda e: e.tensor_scalar(out=q[:], in0=theta, scalar1=shift, scalar2=1.0 / TWO_PI, op0=ALU.add, op1=ALU.mult),
             reads=[tb], writes=[qb])
        S.op("dve", lambda e: e.tensor_copy(out=qi[:], in_=q[:]), reads=[qb], writes=[qib])
        S.op("dve", lambda e: e.tensor_copy(out=q[:], in_=qi[:]), reads=[qib], writes=[qb])
        S.op("dve", lambda e: e.tensor_scalar(out=q[:], in0=q[:], scalar1=-TWO_PI, scalar2=shift, op0=ALU.mult, op1=ALU.add),
             reads=[qb], writes=[qb])
        S.op("dve", lambda e: e.tensor_tensor(out=q[:], in0=q[:], in1=theta, op=ALU.add), reads=[qb, tb], writes=[qb])
        S.op("dve", lambda e: e.tensor_scalar(out=q[:], in0=q[:], scalar1=-math.pi, scalar2=math.pi, op0=ALU.max, op1=ALU.min),
             reads=[qb], writes=[qb])
        S.op("act", lambda e, out=out: e.activation(out=out, in_=q[:], func=AF.Sin), reads=[qb], writes=[ob])


def s5_phase(cx, hT_in, P, consts, gT_out, T, after_setup=None):
    nc, S = cx.nc, cx.S
    NCH = T // 128
    with ExitStack() as es:
        Lr = SB(nc, es, ("Lr", [128, 8, 512], F32))
        Li = SB(nc, es, ("Li", [128, 8, 512], F32))
        Linr = SB(nc, es, ("Linr", [128, 8, 512], BF16))
        Lini = SB(nc, es, ("Lini", [128, 8, 512], BF16))
        MA = SB(nc, es, ("MA", [128, 8, 1024], BF16))
        CP = SB(nc, es, ("CP", [128, 8, 8, 128], BF16))
        dsk = SB(nc, es, ("dsk", [128, 8], F32))
        tri = SB(nc, es, ("tri", [128, 128], BF16))
        self_ = SB(nc, es, ("self", [128, 128], F32))
        msk = SB(nc, es, ("msk", [128, 512], F32))
        sp1 = SB(nc, es, ("sp1", [128, 2], F32))
        tab_b, ma_b, cp_b, cst_b = Buf("tab"), Buf("MA"), Buf("CP"), Buf("cst")
        S.dma("pool", tri[:], consts[:, C_TRI:C_TRI + 128], writes=[cst_b])
        S.dma("sp", self_[:], consts[:, C_SEL:C_SEL + 128], writes=[cst_b])
        S.dma("sp", msk[:], consts[:, C_MASK:C_MASK + 512], writes=[cst_b])
        S.dma("sp", sp1[:], consts[:, C_SP1:C_SP1 + 2], writes=[cst_b])
        with nc.allow_non_contiguous_dma(reason="tiny parameter vectors"):
            S.dma("sp", dsk[:], P["d"].rearrange("(k p) -> p k", p=128), writes=[cst_b], allow_slow_non_contiguous=True)
        if after_setup is not None:
            after_setup()
        with ExitStack() as ts:
            AR = SB(nc, ts, ("AR", [128, 4096], F32))
            AI = SB(nc, ts, ("AI", [128, 4096], F32))
            DT = SB(nc, ts, ("DT", [128, 64], F32))
            MG = SB(nc, ts, ("MG", [128, 4096], F32))
            SN = SB(nc, ts, ("SN", [128, 4096], F32))
            CS = SB(nc, ts, ("CS", [128, 4096], F32))
            ab, ib, db, mb, sb_ = Buf("AR"), Buf("AI"), Buf("DT"), Buf("MG"), Buf("SC")
            S.dma("sp", AR[:], P["a_re"].rearrange("g p -> (g p)").partition_broadcast(128), writes=[ab])
            S.dma("sp", AI[:], P["a_im"].rearrange("g p -> (g p)").partition_broadcast(128), writes=[ib])
            S.dma("sp", DT[:], P["log_dt"].partition_broadcast(128), writes=[db])
            S.op("act", lambda e: e.activation(out=DT[:], in_=DT[:], func=AF.Exp), reads=[db], writes=[db])
            dtb = DT[:].unsqueeze(2).to_broadcast([128, 64, 64])
            v3 = lambda t: t[:].rearrange("p (g q) -> p g q", q=64)
            S.op("dve", lambda e: e.tensor_tensor(out=v3(AR), in0=v3(AR), in1=dtb, op=ALU.mult), reads=[ab, db], writes=[ab])
            S.op("dve", lambda e: e.tensor_tensor(out=v3(AI), in0=v3(AI), in1=dtb, op=ALU.mult), reads=[ib, db], writes=[ib])
            S.op("dve", lambda e: e.tensor_scalar(out=AI[:], in0=AI[:], scalar1=sp1[:, 0:1], scalar2=None, op0=ALU.mult),
                 reads=[ib, cst_b], writes=[ib])
            sincos(cx, ts, AI[:], ib, 4096, SN[:], CS[:], sb_)
            fl = lambda t: t[:].rearrange("p k q -> p (k q)")
            S.op("act", lambda e: e.activation(out=MG[:], in_=AR[:], func=AF.Exp, scale=sp1[:, 0:1]), reads=[ab, cst_b], writes=[mb])
            S.op("dve", lambda e: e.tensor_tensor(out=fl(Lr), in0=MG[:], in1=CS[:], op=ALU.mult), reads=[mb, sb_], writes=[tab_b])
            S.op("dve", lambda e: e.tensor_tensor(out=fl(Li), in0=MG[:], in1=SN[:], op=ALU.mult), reads=[mb, sb_], writes=[tab_b])
            S.op("act", lambda e: e.activation(out=MG[:], in_=AR[:], func=AF.Exp, scale=sp1[:, 1:2]), reads=[ab, cst_b], writes=[mb])
            S.op("dve", lambda e: e.tensor_tensor(out=fl(Linr), in0=MG[:], in1=CS[:], op=ALU.mult), reads=[mb, sb_], writes=[tab_b])
            S.op("dve", lambda e: e.scalar_tensor_tensor(out=fl(Lini), in0=MG[:], scalar=-1.0, in1=SN[:], op0=ALU.mult, op1=ALU.mult),
                 reads=[mb, sb_], writes=[tab_b])
            S.barrier()
        with ExitStack() as ts:
            a2 = SB(nc, ts, ("a2", [128, 8, 32], F32))
            sc = SB(nc, ts, ("sc2", [128, 2, 32], F32))
            BR = SB(nc, ts, ("BR", [128, 32, 16], F32))
            BI = SB(nc, ts, ("BI", [128, 32, 16], F32))
            BBR = SB(nc, ts, ("BBR", [128, 32, 16], F32))
            BBI = SB(nc, ts, ("BBI", [128, 32, 16], F32))
            TM = SB(nc, ts, ("TM", [128, 32, 16], F32))
            bbP = [SB(nc, ts, ("bbP%d" % i, [128, 128], F32)) for i in range(2)]
            CC = [SB(nc, ts, ("CC%d" % i, [128, 8, 2, 64], F32)) for i in range(2)]
            a2b, scb, brb, bbb, tmb, ccb = Buf("a2"), Buf("sc"), Buf("BR"), Buf("BB"), Buf("TM"), Buf("CC")
            bbPb = [Buf("bbP0"), Buf("bbP1")]
            with nc.allow_non_contiguous_dma(reason="tiny parameter vectors"):
                S.dma("sp", a2[:, 0, :], P["a_re"].rearrange("g p -> (g p)").rearrange("(t q) -> q t", q=128), writes=[a2b],
                      allow_slow_non_contiguous=True)
                S.dma("sp", a2[:, 1, :], P["a_im"].rearrange("g p -> (g p)").rearrange("(t q) -> q t", q=128), writes=[a2b],
                      allow_slow_non_contiguous=True)
                ldv = P["log_dt"].rearrange("(t two) -> two t", two=2)
                for g2 in range(2):
                    S.dma("sp", a2[g2 * 64:(g2 + 1) * 64, 2, :], ldv[g2].partition_broadcast(64), writes=[a2b],
                          allow_slow_non_contiguous=True)
            S.dma("sp", BR[:], P["b_re"].rearrange("g p c -> (g p) c").rearrange("(t q) c -> q t c", q=128), writes=[brb])
            S.dma("sp", BI[:], P["b_im"].rearrange("g p c -> (g p) c").rearrange("(t q) c -> q t c", q=128), writes=[brb])
            for i, nm in enumerate(("c_re", "c_im")):
                cv = P[nm].rearrange("(k gl) c p -> (gl c) k p", gl=8)
                for dup in range(2):
                    S.dma("sp", CC[i][:, :, dup, :], cv, writes=[ccb])
            A = lambda i: a2[:, i, :]
            ops = S.op
            ops("act", lambda e: e.activation(out=A(2), in_=A(2), func=AF.Exp), reads=[a2b], writes=[a2b])
            ops("dve", lambda e: e.tensor_tensor(out=A(3), in0=A(0), in1=A(2), op=ALU.mult), reads=[a2b], writes=[a2b])
            ops("dve", lambda e: e.tensor_tensor(out=A(4), in0=A(1), in1=A(2), op=ALU.mult), reads=[a2b], writes=[a2b])
            ops("act", lambda e: e.activation(out=A(3), in_=A(3), func=AF.Exp), reads=[a2b], writes=[a2b])
            sincos(cx, ts, A(4), a2b, 32, sc[:, 0, :], sc[:, 1, :], scb)
            ops("dve", lambda e: e.tensor_tensor(out=A(5), in0=A(3), in1=sc[:, 0, :], op=ALU.mult), reads=[a2b, scb], writes=[a2b])
            ops("dve", lambda e: e.tensor_tensor(out=A(3), in0=A(3), in1=sc[:, 1, :], op=ALU.mult), reads=[a2b, scb], writes=[a2b])
            ops("dve", lambda e: e.tensor_scalar(out=A(3), in0=A(3), scalar1=-1.0, scalar2=None, op0=ALU.add), reads=[a2b], writes=[a2b])
            ops("dve", lambda e: e.tensor_tensor(out=A(2), in0=A(0), in1=A(0), op=ALU.mult), reads=[a2b], writes=[a2b])
            ops("dve", lambda e: e.tensor_tensor(out=A(4), in0=A(1), in1=A(1), op=ALU.mult), reads=[a2b], writes=[a2b])
            ops("dve", lambda e: e.tensor_tensor(out=A(2), in0=A(2), in1=A(4), op=ALU.add), reads=[a2b], writes=[a2b])
            ops("dve", lambda e: e.reciprocal(out=A(2), in_=A(2)), reads=[a2b], writes=[a2b])
            ops("dve", lambda e: e.tensor_tensor(out=A(6), in0=A(3), in1=A(0), op=ALU.mult), reads=[a2b], writes=[a2b])
            ops("dve", lambda e: e.tensor_tensor(out=A(4), in0=A(5), in1=A(1), op=ALU.mult), reads=[a2b], writes=[a2b])
            ops("dve", lambda e: e.tensor_tensor(out=A(6), in0=A(6), in1=A(4), op=ALU.add), reads=[a2b], writes=[a2b])
            ops("dve", lambda e: e.tensor_tensor(out=A(6), in0=A(6), in1=A(2), op=ALU.mult), reads=[a2b], writes=[a2b])
            ops("dve", lambda e: e.tensor_tensor(out=A(7), in0=A(5), in1=A(0), op=ALU.mult), reads=[a2b], writes=[a2b])
            ops("dve", lambda e: e.tensor_tensor(out=A(4), in0=A(3), in1=A(1), op=ALU.mult), reads=[a2b], writes=[a2b])
            ops("dve", lambda e: e.tensor_tensor(out=A(7), in0=A(7), in1=A(4), op=ALU.subtract), reads=[a2b], writes=[a2b])
            ops("dve", lambda e: e.tensor_tensor(out=A(7), in0=A(7), in1=A(2), op=ALU.mult), reads=[a2b], writes=[a2b])
            zrb = a2[:, 6, :].unsqueeze(2).to_broadcast([128, 32, 16])
            zib = a2[:, 7, :].unsqueeze(2).to_broadcast([128, 32, 16])
            ops("dve", lambda e: e.tensor_tensor(out=BBR[:], in0=BR[:], in1=zrb, op=ALU.mult), reads=[a2b, brb], writes=[bbb])
            ops("dve", lambda e: e.tensor_tensor(out=TM[:], in0=BI[:], in1=zib, op=ALU.mult), reads=[a2b, brb], writes=[tmb])
            ops("dve", lambda e: e.tensor_tensor(out=BBR[:], in0=BBR[:], in1=TM[:], op=ALU.subtract), reads=[tmb], writes=[bbb])
            ops("dve", lambda e: e.tensor_tensor(out=BBI[:], in0=BI[:], in1=zrb, op=ALU.mult), reads=[a2b, brb], writes=[bbb])
            ops("dve", lambda e: e.tensor_tensor(out=TM[:], in0=BR[:], in1=zib, op=ALU.mult), reads=[a2b, brb, bbb], writes=[tmb])
            ops("dve", lambda e: e.tensor_tensor(out=BBI[:], in0=BBI[:], in1=TM[:], op=ALU.add), reads=[tmb], writes=[bbb])
            n = 0
            for t in range(32):
                k, j = divmod(t, 4)
                mj = msk[:, j * 128:(j + 1) * 128].rearrange("p (g c) -> p g c", c=16)
                for ri, BB in enumerate((BBR, BBI)):
                    i = n % 2
                    n += 1
                    bank = i
                    src = BB[:, t, :].unsqueeze(1).to_broadcast([128, 8, 16])
                    ops("dve", lambda e: e.tensor_tensor(out=bbP[i][:].rearrange("p (g c) -> p g c", c=16), in0=mj, in1=src, op=ALU.mult),
                        reads=[bbb, cst_b], writes=[bbPb[i]])
                    ops("pe", lambda e: e.transpose(out=cx.ps[bank][:, 0:128], in_=bbP[i][:], identity=cx.identf[:]),
                        reads=[bbPb[i], cx.identf_b], writes=[cx.psb[bank]])
                    ops("act", lambda e: e.copy(out=MA[:, k, ri * 512 + j * 128: ri * 512 + (j + 1) * 128], in_=cx.ps[bank][:, 0:128]),
                        reads=[cx.psb[bank]], writes=[ma_b])
            for k in range(8):
                for ci in range(2):
                    bank = 2 + ci
                    ops("pe", lambda e: e.transpose(out=cx.ps[bank][:, 0:128], in_=CC[ci][:, k].rearrange("p a q -> p (a q)"),
                                                    identity=cx.identf[:]),
                        reads=[ccb, cx.identf_b], writes=[cx.psb[bank]])
                    for j in range(4):
                        ops("dve", lambda e: e.scalar_tensor_tensor(out=CP[:, k, ci * 4 + j, :], in0=cx.ps[bank][:, 0:128],
                                                                    scalar=(1.0 if ci == 0 else -1.0),
                                                                    in1=msk[:, j * 128:(j + 1) * 128], op0=ALU.mult, op1=ALU.mult),
                            reads=[cx.psb[bank], cst_b], writes=[cp_b])
            S.barrier()
        uT = [SB(nc, es, ("s5uT%d" % i, [128, 8, 512], BF16)) for i in range(2)]
        uT_b = [Buf("uT0"), Buf("uT1")]
        M1 = [SB(nc, es, ("M1_%d" % i, [128, 1024], BF16)) for i in range(2)]
        M2 = [SB(nc, es, ("M2_%d" % i, [128, 1024], BF16)) for i in range(2)]
        M_b = [Buf("M0"), Buf("M1")]
        X = SB(nc, es, ("X", [128, 8, 1024], F32))
        X_b = [Buf("X%d" % k) for k in range(8)]
        N2 = [SB(nc, es, ("N2_%d" % i, [128, 1024], F32)) for i in range(2)]
        N2_b = [Buf("N2a"), Buf("N2b")]
        XTs = [SB(nc, es, ("XTs%d" % i, [128, 1024], BF16)) for i in range(2)]
        XTs_b = [Buf("XTs0"), Buf("XTs1")]
        vt = [SB(nc, es, ("vt%d" % i, [128, 128], F32)) for i in range(2)]
        vt_b = [Buf("vt0"), Buf("vt1")]
        GTs = [SB(nc, es, ("GTs%d" % i, [128, 8, 512], BF16)) for i in range(2)]
        GTs_b = [Buf("GT0"), Buf("GT1")]
        BU, W, XT, YT = (0, 1), (2, 3), (4, 5), 6
        S.dma("sp", uT[0][:], hT_in[0], writes=[uT_b[0]])
        items = [(j, k) for j in range(NCH) for k in range(8)]

        def st0(it):
            j, k = it
            J, tt = divmod(j, 4)
            if k == 0 and tt == 1 and J + 1 < T // 512:
                S.dma("sp", uT[(J + 1) % 2][:], hT_in[J + 1], writes=[uT_b[(J + 1) % 2]])
            u = uT[J % 2][:, k, tt * 128:(tt + 1) * 128]
            for ri in range(2):
                S.op("pe", lambda e: e.matmul(cx.ps[BU[ri]][:], lhsT=u, rhs=MA[:, k, ri * 512:(ri + 1) * 512], start=True, stop=True),
                     reads=[uT_b[J % 2], ma_b], writes=[cx.psb[BU[ri]]])

        def st1(it):
            j, k = it
            i = (j * 8 + k) % 2
            bur, bui = cx.ps[BU[0]][:], cx.ps[BU[1]][:]
            rd = [tab_b]
            S.op("dve", lambda e: e.tensor_tensor(out=M1[i][:, 0:512], in0=bur, in1=Linr[:, k, :], op=ALU.mult),
                 reads=rd + [cx.psb[BU[0]]], writes=[M_b[i]])
            S.op("dve", lambda e: e.tensor_tensor(out=M2[i][:, 512:1024], in0=bur, in1=Lini[:, k, :], op=ALU.mult),
                 reads=rd + [cx.psb[BU[0]]], writes=[M_b[i]])
            S.op("dve", lambda e: e.tensor_tensor(out=M1[i][:, 512:1024], in0=bui, in1=Linr[:, k, :], op=ALU.mult),
                 reads=rd + [cx.psb[BU[1]]], writes=[M_b[i]])
            S.op("dve", lambda e: e.scalar_tensor_tensor(out=M2[i][:, 0:512], in0=bui, scalar=-1.0, in1=Lini[:, k, :],
                                                         op0=ALU.mult, op1=ALU.mult),
                 reads=rd + [cx.psb[BU[1]]], writes=[M_b[i]])

        def st2(it):
            j, k = it
            i = (j * 8 + k) % 2
            for ri in range(2):
                sl = slice(ri * 512, (ri + 1) * 512)
                S.op("pe", lambda e: e.matmul(cx.ps[W[ri]][:], lhsT=tri[:], rhs=M1[i][:, sl], start=True, stop=False),
                     reads=[M_b[i], cst_b], writes=[cx.psb[W[ri]]])
                S.op("pe", lambda e: e.matmul(cx.ps[W[ri]][:], lhsT=tri[:], rhs=M2[i][:, sl], start=False, stop=(j == 0)),
                     reads=[M_b[i], cst_b], writes=[cx.psb[W[ri]]])
                if j > 0:
                    S.op("pe", lambda e: e.matmul(cx.ps[W[ri]][:], lhsT=self_[:], rhs=X[:, k, sl], start=False, stop=True),
                         reads=[X_b[k], cst_b], writes=[cx.psb[W[ri]]])

        def st3(it):
            j, k = it
            i = (j * 8 + k) % 2
            wr, wi = cx.ps[W[0]][:], cx.ps[W[1]][:]
            S.op("dve", lambda e: e.tensor_tensor(out=X[:, k, 0:512], in0=wr, in1=Lr[:, k, :], op=ALU.mult),
                 reads=[tab_b, cx.psb[W[0]]], writes=[X_b[k]])
            S.op("dve", lambda e: e.tensor_tensor(out=N2[i][:, 512:1024], in0=wr, in1=Li[:, k, :], op=ALU.mult),
                 reads=[tab_b, cx.psb[W[0]]], writes=[N2_b[i]])
            S.op("dve", lambda e: e.tensor_tensor(out=X[:, k, 512:1024], in0=wi, in1=Lr[:, k, :], op=ALU.mult),
                 reads=[tab_b, cx.psb[W[1]]], writes=[X_b[k]])
            S.op("dve", lambda e: e.scalar_tensor_tensor(out=N2[i][:, 0:512], in0=wi, scalar=-1.0, in1=Li[:, k, :],
                                                         op0=ALU.mult, op1=ALU.mult),
                 reads=[tab_b, cx.psb[W[1]]], writes=[N2_b[i]])
            S.op("pool", lambda e: e.tensor_tensor(out=X[:, k, :], in0=X[:, k, :], in1=N2[i][:], op=ALU.add),
                 reads=[N2_b[i]], writes=[X_b[k]])

        def st4(it):
            j, k = it
            for b in range(8):
                bank = XT[b // 4]
                S.op("pe", lambda e: e.transpose(out=cx.ps[bank][:, (b % 4) * 128:(b % 4 + 1) * 128], in_=X[:, k, b * 128:(b + 1) * 128],
                                                 identity=cx.identf[:]),
                     reads=[X_b[k], cx.identf_b], writes=[cx.psb[bank]])

        def st5(it):
            j, k = it
            i = (j * 8 + k) % 2
            for hb in range(2):
                S.op("act", lambda e: e.copy(out=XTs[i][:, hb * 512:(hb + 1) * 512], in_=cx.ps[XT[hb]][:]),
                     reads=[cx.psb[XT[hb]]], writes=[XTs_b[i]])

        def st6(it):
            j, k = it
            i = (j * 8 + k) % 2
            for b in range(8):
                S.op("pe", lambda e: e.matmul(cx.ps[YT][:, 0:128], lhsT=CP[:, k, b, :], rhs=XTs[i][:, b * 128:(b + 1) * 128],
                                              start=(b == 0), stop=(b == 7)),
                     reads=[cp_b, XTs_b[i]], writes=[cx.psb[YT]])

        def st7(it):
            j, k = it
            i = (j * 8 + k) % 2
            J, tt = divmod(j, 4)
            u = uT[J % 2][:, k, tt * 128:(tt + 1) * 128]
            S.op("dve", lambda e: e.scalar_tensor_tensor(out=vt[i][:], in0=u, scalar=dsk[:, k:k + 1], in1=cx.ps[YT][:, 0:128],
                                                         op0=ALU.mult, op1=ALU.add),
                 reads=[uT_b[J % 2], cst_b, cx.psb[YT]], writes=[vt_b[i]])
            S.op("act", lambda e: e.activation(out=GTs[J % 2][:, k, tt * 128:(tt + 1) * 128], in_=vt[i][:], func=AF.Gelu),
                 reads=[vt_b[i]], writes=[GTs_b[J % 2]])
            if k == 7 and tt == 3:
                S.dma("sp", gT_out[J], GTs[J % 2][:], reads=[GTs_b[J % 2]])

        pipeline([st0, st1, st2, st3, st4, st5, st6, st7], items)
    S.barrier()


def proj_phase(cx, h_in, aT_in, w_ap, N, glu, g_ap, b_ap, h_out, hT_out, T, after_setup=None):
    nc, S = cx.nc, cx.S
    NT = T // 512
    with ExitStack() as es:
        Wt = SB(nc, es, ("Wp", [128, 8, N], BF16))
        w_b = Buf("Wp")
        wv = w_ap.rearrange("(c p) n -> p c n", p=128)
        for c in range(8):
            S.dma("pool", Wt[:, c, :], wv[:, c, :], writes=[w_b])
        if after_setup is not None:
            after_setup()
        gbt = ln_epilogue_setup(cx, es, g_ap, b_ap)
        epi = Epi(cx, es, gbt, DN_ALPHA, LN_EPS, h_in, h_out, hT_out, tr_banks=[6, 7], nb=4)
        aT = [SB(nc, es, ("paT%d" % i, [128, 8, 512], BF16)) for i in range(2)]
        aT_b = [Buf("aT0"), Buf("aT1")]
        sg = [SB(nc, es, ("psg%d" % i, [128, 1024], F32)) for i in range(2)]
        sg_b = [Buf("psg0"), Buf("psg1")]
        pending = []
        S.dma("sp", aT[0][:], aT_in[0], writes=[aT_b[0]])
        n = 0
        for n_ in range(min(2, NT * 4)):
            epi.prefetch_n(n_)
        for J in range(NT):
            cur = J % 2
            if J + 1 < NT:
                S.dma("sp", aT[1 - cur][:], aT_in[J + 1], writes=[aT_b[1 - cur]])
            for tt in range(4):
                tok0 = J * 512 + tt * 128
                if J * 4 + tt + 2 < NT * 4:
                    epi.prefetch_n(J * 4 + tt + 2)
                nb = N // 512
                pb = 0 if glu else 2 * ((J * 4 + tt) % 2)
                for nbk in range(nb):
                    for c in range(8):
                        S.op("pe", lambda e: e.matmul(cx.ps[pb + nbk][:], lhsT=aT[cur][:, c, tt * 128:(tt + 1) * 128],
                                                      rhs=Wt[:, c, nbk * 512:(nbk + 1) * 512], start=(c == 0), stop=(c == 7)),
                             reads=[aT_b[cur], w_b], writes=[cx.psb[pb + nbk]])
                flush(pending, keep=2)
                i = epi.k % epi.nb
                hbuf = epi.h[i]
                if glu:
                    k = n % 2
                    n += 1
                    for hf in range(2):
                        sl = slice(hf * 512, (hf + 1) * 512)
                        S.op("act", lambda e: e.activation(out=sg[k][:, sl], in_=cx.ps[2 + hf][:], func=AF.Sigmoid),
                             reads=[cx.psb[2 + hf]], writes=[sg_b[k]])
                        S.op("dve", lambda e: e.tensor_tensor(out=sg[k][:, sl], in0=cx.ps[hf][:], in1=sg[k][:, sl], op=ALU.mult),
                             reads=[cx.psb[hf]], writes=[sg_b[k]])
                    S.op("dve", lambda e: e.scalar_tensor_tensor(out=hbuf[:], in0=hbuf[:], scalar=epi.kres, in1=sg[k][:],
                                                                 op0=ALU.mult, op1=ALU.add),
                         reads=[sg_b[k]], writes=[epi.b["h"][i]])
                else:
                    for hf in range(2):
                        sl = slice(hf * 512, (hf + 1) * 512)
                        S.op("dve", lambda e: e.scalar_tensor_tensor(out=hbuf[:, sl], in0=hbuf[:, sl], scalar=epi.kres, in1=cx.ps[pb + hf][:],
                                                                     op0=ALU.mult, op1=ALU.add),
                             reads=[cx.psb[pb + hf]], writes=[epi.b["h"][i]])
                epi.run_post(tok0, pending)
        flush(pending)
    S.barrier()


CONV_W = 31
NEGV = -30000.0
CO_A, CO_Q, CO_KC, CO_VC, CO_KS, CO_VS, CO_KW, CO_VW, CO_G, CO_END = 0, 1024, 1536, 1664, 1792, 1920, 2048, 2176, 2304, 2328


def e1_phase(cx, hT_in, P, consts, SC, T, after_setup=None):
    nc, S = cx.nc, cx.S
    NT = T // 512
    with ExitStack() as es:
        Win = SB(nc, es, ("eWin", [128, 8, CO_END], BF16))
        win_b = Buf("eWin")
        wv = P["w_in"].rearrange("(c p) f -> p c f", p=128)
        wblocks = ((0, 256), (512, 768), (256, 512), (768, 1024), (1024, 1536), (1536, 2048), (2048, CO_END))
        wbufs = [Buf("eWin%d" % i) for i in range(len(wblocks))]
        for (c0_, c1_), wb_ in zip(wblocks, wbufs):
            S.dma("pool", Win[:, :, c0_:c1_], wv[:, :, c0_:c1_], writes=[wb_])

        def wb(col):
            for (c0_, c1_), wb_ in zip(wblocks, wbufs):
                if c0_ <= col < c1_:
                    return wb_
            raise ValueError(col)
        if after_setup is not None:
            after_setup()
        DG = SB(nc, es, ("DG", [128, 4 * CONV_W, 128], BF16))
        dg_b = Buf("DG")
        cw = SB(nc, es, ("cw", [128, 4, CONV_W], F32))
        cpar = SB(nc, es, ("cpar", [128, 3, 4], F32))
        onesf = SB(nc, es, ("onesf", [128, 128], F32))
        cst_b = Buf("e1cst")
        with nc.allow_non_contiguous_dma(reason="small parameter transposes"):
            for cc in range(4):
                S.dma("sp", cw[:, cc, :], P["conv_w"][:, cc * 128:(cc + 1) * 128].rearrange("k c -> c k"), writes=[cst_b],
                      allow_slow_non_contiguous=True)
            for i, nm in enumerate(("conv_b", "cln_g", "cln_b")):
                S.dma("sp", cpar[:, i, :], P[nm].rearrange("(cc c) -> c cc", c=128), writes=[cst_b], allow_slow_non_contiguous=True)
        S.op("dve", lambda e: e.memset(onesf[:], 1.0), writes=[cst_b])
        for cc in range(4):
            for k in range(CONV_W):
                S.op("dve", lambda e: e.tensor_scalar(out=DG[:, cc * CONV_W + k, :], in0=cx.identf[:], scalar1=cw[:, cc, k:k + 1],
                                                      scalar2=None, op0=ALU.mult),
                     reads=[cst_b, cx.identf_b], writes=[dg_b])
        hT = [SB(nc, es, ("e1hT%d" % i, [128, 8, 512], BF16)) for i in range(2)]
        hT_b = [Buf("e1hT0"), Buf("e1hT1")]
        aTb = [SB(nc, es, ("aTb%d" % i, [128, 4, 512 + 30], BF16)) for i in range(2)]
        aTb_b = [Buf("aTb0"), Buf("aTb1")]
        sg = [SB(nc, es, ("e1sg%d" % i, [128, 512], BF16)) for i in range(2)]
        sg_b = [Buf("e1sg0"), Buf("e1sg1")]
        xs = SB(nc, es, ("xs", [128, 4, 512], F32))
        xq = SB(nc, es, ("xq", [128, 4, 512], F32))
        xs_b, xq_b = Buf("xs"), Buf("xq")
        mean = SB(nc, es, ("cmean", [128, 512], F32))
        rstd = SB(nc, es, ("crstd", [128, 512], F32))
        mean_b, rstd_b = Buf("mean"), Buf("rstd")
        yt = [SB(nc, es, ("e1y%d" % i, [128, 512], F32)) for i in range(2)]
        yt_b = [Buf("y0"), Buf("y1")]
        cst = SB(nc, es, ("catst", [128, 4, 512], BF16))
        cst_sb = Buf("catst")
        qst = SB(nc, es, ("qst", [64, 8, 512], BF16))
        kst = SB(nc, es, ("kst", [64, 8, 512], BF16))
        qst_b, kst_b = Buf("qst"), Buf("kst")
        vst = SB(nc, es, ("vst", [128, 4, 256], BF16))
        gst = SB(nc, es, ("gst", [128, 4, 24], F32))
        vst_b, gst_b = Buf("vst"), Buf("gst")
        S.dma("sp", hT[0][:], hT_in[0], writes=[hT_b[0]])
        nps = [0]

        def bank2():
            nps[0] += 1
            return nps[0] % 2

        for J in range(NT):
            cur = J % 2
            if J + 1 < NT:
                S.dma("sp", hT[1 - cur][:], hT_in[J + 1], writes=[hT_b[1 - cur]])
            ab = aTb[cur]
            if J == 0:
                S.op("pool", lambda e: e.memset(ab[:, :, 0:30], 0.0), writes=[aTb_b[cur]])
            else:
                S.op("pool", lambda e: e.tensor_copy(out=ab[:, :, 0:30], in_=aTb[1 - cur][:, :, 512:542]),
                     reads=[aTb_b[1 - cur]], writes=[aTb_b[cur]])
            for cc in range(4):
                pv, pg = 0 + cc % 2, 2 + cc % 2
                for c in range(8):
                    S.op("pe", lambda e: e.matmul(cx.ps[pv][:], lhsT=Win[:, c, cc * 128:(cc + 1) * 128], rhs=hT[cur][:, c, :],
                                                  start=(c == 0), stop=(c == 7)), reads=[wb(cc * 128), hT_b[cur]], writes=[cx.psb[pv]])
                for c in range(8):
                    S.op("pe", lambda e: e.matmul(cx.ps[pg][:], lhsT=Win[:, c, 512 + cc * 128:512 + (cc + 1) * 128], rhs=hT[cur][:, c, :],
                                                  start=(c == 0), stop=(c == 7)), reads=[wb(512 + cc * 128), hT_b[cur]], writes=[cx.psb[pg]])
                k = cc % 2
                S.op("act", lambda e: e.activation(out=sg[k][:], in_=cx.ps[pg][:], func=AF.Sigmoid), reads=[cx.psb[pg]], writes=[sg_b[k]])
                S.op("dve", lambda e: e.tensor_tensor(out=ab[:, cc, 30:542], in0=cx.ps[pv][:], in1=sg[k][:], op=ALU.mult),
                     reads=[cx.psb[pv], sg_b[k]], writes=[aTb_b[cur]])
            chunks = [(CO_Q + i * 128, "q", i) for i in range(4)] + [(CO_KC, "k", 0), (CO_VC, "k", 1), (CO_KS, "k", 2), (CO_KW, "k", 3)]
            for col, kind, idx in chunks:
                bk = 4 + bank2()
                for c in range(8):
                    S.op("pe", lambda e: e.matmul(cx.ps[bk][:], lhsT=Win[:, c, col:col + 128], rhs=hT[cur][:, c, :],
                                                  start=(c == 0), stop=(c == 7)), reads=[wb(col), hT_b[cur]], writes=[cx.psb[bk]])
                if kind == "q":
                    S.op("act", lambda e: e.mul(out=qst[:, 2 * idx, :], in_=cx.ps[bk][0:64, :], mul=0.125),
                         reads=[cx.psb[bk]], writes=[qst_b])
                    S.op("dve", lambda e: e.tensor_scalar(out=qst[:, 2 * idx + 1, :], in0=cx.ps[bk][64:128, :], scalar1=0.125,
                                                          scalar2=None, op0=ALU.mult), reads=[cx.psb[bk]], writes=[qst_b])
                else:
                    S.op("act", lambda e: e.copy(out=kst[:, 2 * idx, :], in_=cx.ps[bk][0:64, :]), reads=[cx.psb[bk]], writes=[kst_b])
                    S.op("dve", lambda e: e.tensor_copy(out=kst[:, 2 * idx + 1, :], in_=cx.ps[bk][64:128, :]),
                         reads=[cx.psb[bk]], writes=[kst_b])
            S.dma("sp", SC.QA[:, 0:64, J * 512:(J + 1) * 512].rearrange("h p t -> p h t"), qst[:], reads=[qst_b])
            S.dma("sp", SC.KX[:, :, J * 512:(J + 1) * 512].rearrange("n p t -> p n t"), kst[:], reads=[kst_b])
            for sub in range(4):
                bk = 6 + sub % 2
                for c in range(8):
                    S.op("pe", lambda e: e.matmul(cx.ps[bk][:, 0:128], lhsT=hT[cur][:, c, sub * 128:(sub + 1) * 128],
                                                  rhs=Win[:, c, CO_VS:CO_VS + 128], start=(c == 0), stop=(c == 7)),
                         reads=[wb(CO_VS), hT_b[cur]], writes=[cx.psb[bk]])
                for c in range(8):
                    S.op("pe", lambda e: e.matmul(cx.ps[bk][:, 128:280], lhsT=hT[cur][:, c, sub * 128:(sub + 1) * 128],
                                                  rhs=Win[:, c, CO_VW:CO_END], start=(c == 0), stop=(c == 7)),
                         reads=[wb(CO_VW), hT_b[cur]], writes=[cx.psb[bk]])
                S.op("dve", lambda e: e.tensor_copy(out=vst[:, sub, :], in_=cx.ps[bk][:, 0:256]), reads=[cx.psb[bk]], writes=[vst_b])
                S.op("act", lambda e: e.activation(out=gst[:, sub, :], in_=cx.ps[bk][:, 256:280], func=AF.Sigmoid),
                     reads=[cx.psb[bk]], writes=[gst_b])
            S.dma("sp", SC.vtok[J * 512:(J + 1) * 512, :].rearrange("(s p) c -> p s c", p=128), vst[:], reads=[vst_b])
            S.dma("sp", SC.gates[J * 512:(J + 1) * 512, :].rearrange("(s p) c -> p s c", p=128), gst[:], reads=[gst_b])
            for cc in range(4):
                bk = 0 + cc % 2
                for k in range(CONV_W):
                    S.op("pe", lambda e: e.matmul(cx.ps[bk][:], lhsT=DG[:, cc * CONV_W + k, :], rhs=ab[:, cc, k:k + 512],
                                                  start=(k == 0), stop=(k == CONV_W - 1)), reads=[dg_b, aTb_b[cur]], writes=[cx.psb[bk]])
                S.op("act", lambda e: e.activation(out=xs[:, cc, :], in_=cx.ps[bk][:], func=AF.Identity, bias=cpar[:, 0, cc:cc + 1]),
                     reads=[cx.psb[bk], cst_b], writes=[xs_b])
                S.op("act", lambda e: e.activation(out=xq[:, cc, :], in_=xs[:, cc, :], func=AF.Square), reads=[xs_b], writes=[xq_b])
            for cc in range(4):
                S.op("pe", lambda e: e.matmul(cx.ps[2][:], lhsT=onesf[:], rhs=xs[:, cc, :], start=(cc == 0), stop=(cc == 3)),
                     reads=[cst_b, xs_b], writes=[cx.psb[2]])
            for cc in range(4):
                S.op("pe", lambda e: e.matmul(cx.ps[3][:], lhsT=onesf[:], rhs=xq[:, cc, :], start=(cc == 0), stop=(cc == 3)),
                     reads=[cst_b, xq_b], writes=[cx.psb[3]])
            S.op("dve", lambda e: e.tensor_scalar(out=mean[:], in0=cx.ps[2][:], scalar1=1.0 / 512, scalar2=None, op0=ALU.mult),
                 reads=[cx.psb[2]], writes=[mean_b])
            S.op("dve", lambda e: e.tensor_tensor(out=rstd[:], in0=mean[:], in1=mean[:], op=ALU.mult), reads=[mean_b], writes=[rstd_b])
            S.op("dve", lambda e: e.scalar_tensor_tensor(out=rstd[:], in0=cx.ps[3][:], scalar=1.0 / 512, in1=rstd[:],
                                                         op0=ALU.mult, op1=ALU.subtract), reads=[cx.psb[3]], writes=[rstd_b])
            S.op("dve", lambda e: e.tensor_scalar(out=rstd[:], in0=rstd[:], scalar1=LN_EPS, scalar2=None, op0=ALU.add),
                 writes=[rstd_b])
            S.op("act", lambda e: e.activation(out=rstd[:], in_=rstd[:], func=AF.Sqrt), writes=[rstd_b])
            S.op("dve", lambda e: e.reciprocal(out=rstd[:], in_=rstd[:]), writes=[rstd_b])
            for cc in range(4):
                k = cc % 2
                S.op("dve", lambda e: e.tensor_tensor(out=yt[k][:], in0=xs[:, cc, :], in1=mean[:], op=ALU.subtract),
                     reads=[xs_b, mean_b], writes=[yt_b[k]])
                S.op("dve", lambda e: e.tensor_tensor(out=yt[k][:], in0=yt[k][:], in1=rstd[:], op=ALU.mult),
                     reads=[rstd_b], writes=[yt_b[k]])
                S.op("act", lambda e: e.activation(out=cst[:, cc, :], in_=yt[k][:], func=AF.Silu, scale=cpar[:, 1, cc:cc + 1],
                                                   bias=cpar[:, 2, cc:cc + 1]), reads=[yt_b[k], cst_b], writes=[cst_sb])
            S.dma("sp", SC.catT[J][:, 0:4, :], cst[:], reads=[cst_sb])
    S.barrier()


def make_att_consts(T):
    import ml_dtypes
    bf = ml_dtypes.bfloat16
    pos = np.arange(T)
    qe = np.zeros((8, 4, T), np.float32)
    for hh in range(8):
        sl = 2.0 ** (-(hh + 1))
        qe[hh, 0] = sl * 128
        qe[hh, 1] = sl
        qe[hh, 2] = -sl * 128 * (pos // 128)
        qe[hh, 3] = -sl * (pos % 128)
    ke = np.stack([pos // 128, pos % 128, np.ones(T), np.ones(T)]).astype(np.float32)
    NCB = T // 16
    cend = np.arange(NCB) * 16 + 31
    kce = np.stack([cend // 128, cend % 128, np.ones(NCB), np.ones(NCB)]).astype(np.float32)
    nsel = T // 64
    eall = (pos[None, :] // 64 == np.arange(128)[:, None]).astype(np.float32)
    c = np.arange(NCB)
    ov = ((cend[:, None] >= np.arange(128)[None, :] * 64) & ((c * 16)[:, None] < (np.arange(128)[None, :] + 1) * 64)).astype(np.float32)
    nct = (NCB + 127) // 128
    ovp = np.zeros((nct * 128, 128), np.float32)
    ovp[:NCB] = ov
    ovp = ovp.reshape(nct, 128, 128).transpose(1, 0, 2)
    p = np.arange(128)[:, None]
    q = np.arange(512)[None, :]
    cmask = np.zeros((128, 5, 512), np.float32)
    for oi, o in enumerate((-128, -96, -64, -32, 0)):
        cmask[:, oi, :] = np.where(16 * (o + p) + 31 <= q, 0.0, NEGV)
    q1 = np.arange(128)[None, :]
    tri2 = np.zeros((128, 2, 128), np.float32)
    tri2[:, 0, :] = np.where(p > q1, NEGV, 0.0)
    tri2[:, 1, :] = np.where(p <= q1, NEGV, 0.0)
    m = np.arange(254)[None, :] - 126
    cur = (np.arange(128)[:, None] >= 64).astype(np.int64)
    forced = (m == cur) | (m == cur - 1)
    valid = m <= cur
    wm = np.where(valid & ~forced, 1.0, 0.0).astype(np.float32)
    wa = np.where(valid, np.where(forced, 1e4, 0.0), -1.0).astype(np.float32)
    return dict(c_qe=qe.astype(bf), c_ke=ke.astype(bf), c_kce=kce.astype(bf), c_eall=eall.astype(bf), c_ov=ovp.astype(bf),
                c_cmask=cmask.astype(bf), c_tri2=tri2.astype(bf), c_wm=wm, c_wa=wa)


def e3_phase(cx, P, CA, SC, T):
    nc, S = cx.nc, cx.S
    NQT = T // 512
    NKT = T // 128
    NCB = T // 16
    NCT = (NCB + 127) // 128
    with ExitStack() as es:
        KcA = SB(nc, es, ("KcA", [68, 2, NCT * 128], BF16))
        Vc = SB(nc, es, ("Vc", [128, NCT, 2, 193], BF16))
        kc_b, vc_b = Buf("KcA"), Buf("Vc")
        S.op("pool", lambda e: e.memset(KcA[:], 0.0), writes=[kc_b])
        S.op("pool", lambda e: e.memset(Vc[:], 0.0), writes=[vc_b])
        for g in range(2):
            S.dma("sp", KcA[64:68, g, 0:NCB], CA["c_kce"], writes=[kc_b])
            S.dma("sp", Vc[:, :, g, 65:193], CA["c_ov"], writes=[vc_b])
        S.op("pool", lambda e: e.memset(Vc[:, :, :, 64:65], 1.0), writes=[vc_b])
        with ExitStack() as ts:
            kcT = SB(nc, ts, ("kcT", [64, 4, T], BF16))
            kcT_b = Buf("kcT")
            S.dma("sp", kcT[:], SC.KX[0:4].rearrange("n p t -> p n t"), writes=[kcT_b])
            for kv, (w1n, w2n, pen) in enumerate((("w1_k", "w2_k", "pe_k"), ("w1_v", "w2_v", "pe_v"))):
                W1 = SB(nc, ts, ("W1_%d" % kv, [64, 32, 128], BF16))
                W2 = SB(nc, ts, ("W2_%d" % kv, [128, 64], BF16))
                peT = SB(nc, ts, ("peT%d" % kv, [64, 32], BF16))
                bias = SB(nc, ts, ("cb%d" % kv, [128, 1], F32))
                hid = SB(nc, ts, ("hid%d" % kv, [128, NCT * 128], BF16))
                wb, bb, hb = Buf("w1"), Buf("bias"), Buf("hid")
                S.dma("pool", W1[:], P[w1n].rearrange("(j d) h -> d j h", d=64), writes=[wb])
                S.dma("pool", W2[:], P[w2n], writes=[wb])
                with nc.allow_non_contiguous_dma(reason="tiny pe transpose"):
                    S.dma("pool", peT[:], P[pen].rearrange("j d -> d j"), writes=[wb], allow_slow_non_contiguous=True)
                S.op("pool", lambda e: e.memset(hid[:], 0.0), writes=[hb])
                for j in range(32):
                    S.op("pe", lambda e: e.matmul(cx.ps[7][:, 0:1], lhsT=W1[:, j, :], rhs=peT[:, j:j + 1], start=(j == 0), stop=(j == 31)),
                         reads=[wb], writes=[cx.psb[7]])
                S.op("dve", lambda e: e.tensor_copy(out=bias[:], in_=cx.ps[7][:, 0:1]), reads=[cx.psb[7]], writes=[bb])
                for g in range(2):
                    src = kcT[:, kv * 2 + g, :].rearrange("p (n s) -> p n s", s=16)
                    n0 = 0
                    while n0 < NCB - 1:
                        nn = min(512, NCB - 1 - n0)
                        bk = (n0 // 512) % 2
                        for j in range(32):
                            rhs = src[:, n0 + j // 16:n0 + j // 16 + nn, j % 16]
                            S.op("pe", lambda e: e.matmul(cx.ps[bk][:, 0:nn], lhsT=W1[:, j, :], rhs=rhs, start=(j == 0), stop=(j == 31)),
                                 reads=[wb, kcT_b], writes=[cx.psb[bk]])
                        S.op("act", lambda e: e.activation(out=hid[:, n0:n0 + nn], in_=cx.ps[bk][:, 0:nn], func=AF.Silu, bias=bias[:, 0:1]),
                             reads=[cx.psb[bk], bb], writes=[hb])
                        n0 += nn
                    if kv == 0:
                        for n0 in range(0, NCT * 128, 512):
                            nn = min(512, NCT * 128 - n0)
                            S.op("pe", lambda e: e.matmul(cx.ps[2][0:64, 0:nn], lhsT=W2[:], rhs=hid[:, n0:n0 + nn], start=True, stop=True),
                                 reads=[wb, hb], writes=[cx.psb[2]])
                            S.op("dve", lambda e: e.tensor_copy(out=KcA[0:64, g, n0:n0 + nn], in_=cx.ps[2][0:64, 0:nn]),
                                 reads=[cx.psb[2]], writes=[kc_b])
                    else:
                        for ct in range(NCT):
                            S.op("pe", lambda e: e.matmul(cx.ps[3][:, 0:64], lhsT=hid[:, ct * 128:(ct + 1) * 128], rhs=W2[:], start=True, stop=True),
                                 reads=[wb, hb], writes=[cx.psb[3]])
                            S.op("dve", lambda e: e.tensor_copy(out=Vc[:, ct, g, 0:64], in_=cx.ps[3][:, 0:64]),
                                 reads=[cx.psb[3]], writes=[vc_b])
            S.barrier()
        KAs = SB(nc, es, ("KAs", [68, 2, T], BF16))
        KAw = SB(nc, es, ("KAw", [68, 2, T], BF16))
        Vall = SB(nc, es, ("Vall", [128, NKT, 4, 65], BF16))
        EALL = SB(nc, es, ("EALL", [128, T], BF16))
        CM = SB(nc, es, ("CM", [128, 5, 512], BF16))
        TR2 = SB(nc, es, ("TR2", [128, 2, 128], BF16))
        WMA = SB(nc, es, ("WMA", [128, 2, 254], F32))
        kv_b, cst_b = Buf("kv"), Buf("attc")
        for g in range(2):
            S.dma("sp", KAs[0:64, g, :], SC.KX[4 + g], writes=[kv_b])
            S.dma("sp", KAw[0:64, g, :], SC.KX[6 + g], writes=[kv_b])
            S.dma("sp", KAs[64:68, g, :], CA["c_ke"], writes=[kv_b])
            S.dma("sp", KAw[64:68, g, :], CA["c_ke"], writes=[kv_b])
        for v in range(4):
            S.dma("sp", Vall[:, :, v, 0:64], SC.vtok[:, v * 64:(v + 1) * 64].rearrange("(kt p) d -> p kt d", p=128), writes=[kv_b])
        S.op("pool", lambda e: e.memset(Vall[:, :, :, 64:65], 1.0), writes=[kv_b])
        S.dma("sp", EALL[:], CA["c_eall"], writes=[cst_b])
        S.dma("sp", CM[:], CA["c_cmask"], writes=[cst_b])
        S.dma("sp", TR2[:], CA["c_tri2"], writes=[cst_b])
        S.dma("sp", WMA[:, 0, :], CA["c_wm"], writes=[cst_b])
        S.dma("sp", WMA[:, 1, :], CA["c_wa"], writes=[cst_b])
        QAs = [SB(nc, es, ("QAs%d" % i, [68, 4, 512], BF16)) for i in range(2)]
        QAs_b = [Buf("QA0"), Buf("QA1")]
        NPT = 6
        PT = [SB(nc, es, ("PT%d" % i, [128, 512], BF16)) for i in range(NPT)]
        PT_b = [Buf("PT%d" % i) for i in range(NPT)]
        NEGT = [SB(nc, es, ("NEGT%d" % i, [128, 512], BF16)) for i in range(2)]
        NEGT_b = [Buf("NEGT0"), Buf("NEGT1")]
        oacc = SB(nc, es, ("oacc", [128, 4, 512], F32))
        oacc_b = [Buf("oacc%d" % s) for s in range(4)]
        gat = [SB(nc, es, ("gat%d" % i, [128, 4, 24], F32)) for i in range(2)]
        gat_b = [Buf("gat0"), Buf("gat1")]
        imp = SB(nc, es, ("imp", [128, 4, 128], F32))
        imp_b = [Buf("imp%d" % s) for s in range(4)]
        sm = SB(nc, es, ("smalls", [128, 64], F32))
        sm_b = Buf("smalls")
        sc2 = SB(nc, es, ("sc2", [128, 128], F32))
        sc2_b = Buf("sc2")
        negb = SB(nc, es, ("negb", [128, 128], BF16))
        negb_b = Buf("negb")
        ob = SB(nc, es, ("ob", [128, 512], BF16))
        ob_b = Buf("ob")
        ost = SB(nc, es, ("ost", [128, 4, 512], BF16))
        ost_b = Buf("ost")
        ACC = (2, 3, 4, 5)
        cnt = {"st": 0, "pt": 0, "sm": 0, "mx": 0}
        MXs = [SB(nc, es, ("MXs%d" % i, [128, 512], BF16)) for i in range(3)]
        MXs_b = [Buf("MXs%d" % i) for i in range(3)]
        ZB = SB(nc, es, ("ZB", [128, 128], BF16))
        S.op("pool", lambda e: e.memset(ZB[:], 0.0), writes=[cst_b])

        STB = (0, 1, 2, 3)

        def acc_ap(banks, ncols, s_):
            spb = 4 if ncols <= 128 else 2
            bk = banks[s_ // spb]
            return cx.ps[bk][:, (s_ % spb) * ncols:(s_ % spb + 1) * ncols], bk

        def attend(qa, tiles, ncols, banks, filler=False):
            last = {}
            for i, t in enumerate(tiles):
                for s_ in range(t[1] // 128, t[2] // 128):
                    last[s_] = i
            for bk in banks:
                S.op("pe", lambda e: e.matmul(cx.ps[bk][:], lhsT=ZB[:], rhs=EALL[:, 0:512], start=True, stop=True, skip_group_check=True),
                     reads=[cst_b], writes=[cx.psb[bk]])
            pend = []
            for i, (Klhs, c0, c1, extras, V) in enumerate(tiles):
                sb = STB[cnt["st"] % len(STB)]
                cnt["st"] += 1
                pti = cnt["pt"] % NPT
                cnt["pt"] += 1
                S.op("pe", lambda e: e.matmul(cx.ps[sb][:, c0:c1], lhsT=Klhs, rhs=qa[:, c0:c1], start=True, stop=(len(extras) == 0)),
                     reads=[kv_b, kc_b, QAs_b[0], QAs_b[1]], writes=[cx.psb[sb]])
                for xi, (l2, r2, e0, e1) in enumerate(extras):
                    S.op("pe", lambda e: e.matmul(cx.ps[sb][:, e0:e1], lhsT=l2, rhs=r2, start=False, stop=(xi == len(extras) - 1)),
                         reads=[cst_b, cx.ident_b, NEGT_b[0], NEGT_b[1]], writes=[cx.psb[sb]])
                S.op("act", lambda e: e.activation(out=PT[pti][:, c0:c1], in_=cx.ps[sb][:, c0:c1], func=AF.Exp),
                     reads=[cx.psb[sb]], writes=[PT_b[pti]])
                if filler:
                    S.op("pe", lambda e: e.matmul(cx.ps[7][:], lhsT=ZB[:], rhs=EALL[:, 0:512], start=True, stop=True, skip_group_check=True),
                         reads=[cst_b], writes=[cx.psb[7]])

                def pv(i=i, pti=pti, c0=c0, c1=c1, V=V):
                    for s_ in range(c0 // 128, c1 // 128):
                        ap, bk = acc_ap(banks, ncols, s_)
                        S.op("pe", lambda e: e.matmul(ap, lhsT=PT[pti][:, s_ * 128:(s_ + 1) * 128], rhs=V, start=False, stop=(last[s_] == i),
                                                      skip_group_check=True),
                             reads=[PT_b[pti], kv_b, vc_b], writes=[cx.psb[bk]])
                pend.append(pv)
                flush(pend, keep=2)
            flush(pend)

        def finalize(g, j, br, gt, subs, first_branch, with_imp, banks):
            hh = g * 4 + j
            for s in subs:
                k = cnt["sm"] % 16
                cnt["sm"] += 1
                c = sm[:, 4 * k:4 * k + 4]
                acc, bk_ = acc_ap(banks, 193 if with_imp else 65, s)
                accb = cx.psb[bk_]
                S.op("dve", lambda e: e.tensor_scalar(out=c[:, 0:1], in0=acc[:, 64:65], scalar1=1e-30, scalar2=None, op0=ALU.max),
                     reads=[accb], writes=[sm_b])
                S.op("dve", lambda e: e.reciprocal(out=c[:, 1:2], in_=c[:, 0:1]), reads=[sm_b], writes=[sm_b])
                gcol = g * 12 + j * 3 + br
                S.op("dve", lambda e: e.tensor_tensor(out=c[:, 2:3], in0=c[:, 1:2], in1=gt[:, s, gcol:gcol + 1], op=ALU.mult),
                     reads=[sm_b, gat_b[0], gat_b[1]], writes=[sm_b])
                dst = oacc[:, s, hh * 64:(hh + 1) * 64]
                if first_branch:
                    S.op("dve", lambda e: e.tensor_scalar(out=dst, in0=acc[:, 0:64], scalar1=c[:, 2:3], scalar2=None, op0=ALU.mult),
                         reads=[accb, sm_b], writes=[oacc_b[s]])
                else:
                    S.op("dve", lambda e: e.scalar_tensor_tensor(out=dst, in0=acc[:, 0:64], scalar=c[:, 2:3], in1=dst, op0=ALU.mult, op1=ALU.add),
                         reads=[accb, sm_b], writes=[oacc_b[s]])
                if with_imp:
                    if j == 0:
                        S.op("dve", lambda e: e.tensor_scalar(out=imp[:, s, :], in0=acc[:, 65:193], scalar1=c[:, 1:2], scalar2=None, op0=ALU.mult),
                             reads=[accb, sm_b], writes=[imp_b[s]])
                    else:
                        S.op("dve", lambda e: e.scalar_tensor_tensor(out=imp[:, s, :], in0=acc[:, 65:193], scalar=c[:, 1:2], in1=imp[:, s, :],
                                                                     op0=ALU.mult, op1=ALU.add),
                             reads=[accb, sm_b], writes=[imp_b[s]])

        nq = 0
        for qt in range(NQT):
            gt = gat[qt % 2]
            S.dma("sp", gt[:], SC.gates[qt * 512:(qt + 1) * 512, :].rearrange("(s p) c -> p s c", p=128), writes=[gat_b[qt % 2]])
            for g in range(2):
                qi = nq % 2
                nq += 1
                qa4 = QAs[qi]
                S.dma("sp", qa4[:], SC.QA[g * 4:(g + 1) * 4, :, qt * 512:(qt + 1) * 512].rearrange("h p t -> p h t"), writes=[QAs_b[qi]])
                ctmax = min(NCT - 1, (32 * qt + 30) // 128)
                for j in range(4):
                    tiles = []
                    for ct in range(ctmax + 1):
                        o = 128 * ct - 32 * qt
                        extras = []
                        if o + 127 >= -1:
                            oi = (o + 128) // 32
                            extras.append((cx.ident[:], CM[:, oi, :], 0, 512))
                        tiles.append((KcA[:, g, ct * 128:(ct + 1) * 128], 0, 512, extras, Vc[:, ct, g, :]))
                    attend(qa4[:, j, :], tiles, 193, (4, 5), filler=True)
                    finalize(g, j, 0, gt, range(4), True, True, (4, 5))
                for s in range(4):
                    qs = 4 * qt + s
                    off = 126 - 2 * qs
                    S.op("dve", lambda e: e.tensor_tensor(out=sc2[:], in0=imp[:, s, :], in1=WMA[:, 0, off:off + 128], op=ALU.mult),
                         reads=[imp_b[s], cst_b], writes=[sc2_b])
                    S.op("dve", lambda e: e.tensor_tensor(out=sc2[:], in0=sc2[:], in1=WMA[:, 1, off:off + 128], op=ALU.add),
                         reads=[cst_b], writes=[sc2_b])
                    S.op("dve", lambda e: e.memset(sc2[:, 0:1], 1e4), writes=[sc2_b])
                    k = cnt["sm"] % 2
                    cnt["sm"] += 1
                    m8 = sm[:, 32 * k:32 * k + 8]
                    m8b = sm[:, 32 * k + 8:32 * k + 16]
                    S.op("dve", lambda e: e.max(out=m8, in_=sc2[:]), reads=[sc2_b], writes=[sm_b])
                    S.op("dve", lambda e: e.match_replace(out=imp[:, s, :], in_to_replace=m8, in_values=sc2[:], imm_value=-1e9),
                         reads=[sc2_b, sm_b], writes=[imp_b[s]])
                    S.op("dve", lambda e: e.max(out=m8b, in_=imp[:, s, :]), reads=[imp_b[s]], writes=[sm_b])
                    S.op("dve", lambda e: e.tensor_scalar(out=m8b[:, 7:8], in0=m8b[:, 7:8], scalar1=0.0, scalar2=None, op0=ALU.max),
                         reads=[sm_b], writes=[sm_b])
                    S.op("dve", lambda e: e.tensor_scalar(out=negb[:], in0=sc2[:], scalar1=m8b[:, 7:8], scalar2=NEGV, op0=ALU.is_lt, op1=ALU.mult),
                         reads=[sc2_b, sm_b], writes=[negb_b])
                    trv = cx.ps[6][:].bitcast(BF16)
                    S.op("pe", lambda e: e.transpose(out=trv[:, 0:128], in_=negb[:], identity=cx.ident[:]),
                         reads=[negb_b, cx.ident_b], writes=[cx.psb[6]])
                    S.op("act", lambda e: e.copy(out=NEGT[g][:, s * 128:(s + 1) * 128], in_=trv[:, 0:128]),
                         reads=[cx.psb[6]], writes=[NEGT_b[g]])
                for j in range(4):
                    tiles = []
                    for d in range(4):
                        kt = 4 * qt - 4 + d
                        if kt >= 0:
                            tiles.append((KAw[:, g, kt * 128:(kt + 1) * 128], 0, 128 * (d + 1),
                                          [(cx.ident[:], TR2[:, 1, :], 128 * d, 128 * d + 128)], Vall[:, kt, 2 + g, :]))
                    for d in range(4):
                        kt = 4 * qt + d
                        tiles.append((KAw[:, g, kt * 128:(kt + 1) * 128], 128 * d, 512,
                                      [(cx.ident[:], TR2[:, 0, :], 128 * d, 128 * d + 128)], Vall[:, kt, 2 + g, :]))
                    bks = (4 + j % 2,)
                    attend(qa4[:, j, :], tiles, 65, bks, filler=True)
                    finalize(g, j, 2, gt, range(4), False, False, bks)
                for j in range(4):
                    tiles = []
                    for kt in range(4 * qt + 4):
                        d = kt - 4 * qt
                        c0 = 0 if d < 0 else 128 * d
                        extras = [(EALL[:, kt * 128:(kt + 1) * 128], NEGT[g][:, c0:512], c0, 512)]
                        if d >= 0:
                            extras.append((cx.ident[:], TR2[:, 0, :], c0, c0 + 128))
                        tiles.append((KAs[:, g, kt * 128:(kt + 1) * 128], c0, 512, extras, Vall[:, kt, g, :]))
                    bks = (4 + j % 2,)
                    attend(qa4[:, j, :], tiles, 65, bks)
                    finalize(g, j, 1, gt, range(4), False, False, bks)
            for s in range(4):
                S.op("act", lambda e: e.copy(out=ob[:], in_=oacc[:, s, :]), reads=[oacc_b[s]], writes=[ob_b])
                trv = cx.ps[7][:].bitcast(BF16)
                for f in range(4):
                    S.op("pe", lambda e: e.transpose(out=trv[:, f * 128:(f + 1) * 128], in_=ob[:, f * 128:(f + 1) * 128], identity=cx.ident[:]),
                         reads=[ob_b, cx.ident_b], writes=[cx.psb[7]])
                S.op("dve", lambda e: e.tensor_copy(out=ost[:, :, s * 128:(s + 1) * 128], in_=trv[:, 0:512].rearrange("p (f t) -> p f t", f=4)),
                     reads=[cx.psb[7]], writes=[ost_b])
            S.dma("sp", SC.catT[qt][:, 4:8, :], ost[:], reads=[ost_b])
    S.barrier()


def xpose_phase(cx, h_in, hT_out, T):
    nc, S = cx.nc, cx.S
    with ExitStack() as es:
        ht = [SB(nc, es, ("xp_h%d" % i, [128, D], F32)) for i in range(2)]
        hb = [SB(nc, es, ("xp_hb%d" % i, [128, D], BF16)) for i in range(2)]
        st = [SB(nc, es, ("xp_st%d" % i, [128, 8, 512], BF16)) for i in range(2)]
        ht_b, hb_b, st_b = [Buf(), Buf()], [Buf(), Buf()], [Buf(), Buf()]
        for n in range(T // 128):
            i = n % 2
            J, tt = divmod(n, 4)
            S.dma("sp", ht[i][:], h_in[n * 128:(n + 1) * 128, :], writes=[ht_b[i]])
            S.op("dve", lambda e: e.tensor_copy(out=hb[i][:], in_=ht[i][:]), reads=[ht_b[i]], writes=[hb_b[i]])
            bank = 6 + i
            trv = cx.ps[bank][:].bitcast(BF16)
            for c in range(8):
                S.op("pe", lambda e: e.transpose(out=trv[:, c * 128:(c + 1) * 128], in_=hb[i][:, c * 128:(c + 1) * 128], identity=cx.ident[:]),
                     reads=[hb_b[i], cx.ident_b], writes=[cx.psb[bank]])
            S.op("act", lambda e: e.copy(out=st[J % 2][:, :, tt * 128:(tt + 1) * 128], in_=trv.rearrange("p (c t) -> p c t", c=8)),
                 reads=[cx.psb[bank]], writes=[st_b[J % 2]])
            if tt == 3:
                S.dma("sp", hT_out[J], st[J % 2][:], reads=[st_b[J % 2]])
    S.barrier()


EV_NAMES = ["w_in", "conv_w", "conv_b", "cln_g", "cln_b", "pe_k", "w1_k", "w2_k", "pe_v", "w1_v", "w2_v", "w_out"]
OD_NAMES = ["a_re", "a_im", "log_dt", "b_re", "b_im", "c_re", "c_im", "d", "w_glu"]
CA_NAMES = ["c_qe", "c_ke", "c_kce", "c_eall", "c_ov", "c_cmask", "c_tri2", "c_wm", "c_wa"]


def build_program(T, layers=range(DEPTH)):
    nc = bass.Bass("TRN2", target_bir_lowering=False)
    dt = lambda n, sh, ty=F32, kind="ExternalInput": nc.dram_tensor(n, list(sh), ty, kind=kind).ap()
    x = dt("x", [T, D])
    out = dt("out", [T, D], F32, "ExternalOutput")
    W = {}
    W["ffn1_w_in"] = dt("ffn1_w_in", [DEPTH, D, 2 * FF])
    W["ffn1_w_out"] = dt("ffn1_w_out", [DEPTH, FF, D])
    W["ffn2_w_in"] = dt("ffn2_w_in", [DEPTH, D, 2 * FF])
    W["ffn2_w_out"] = dt("ffn2_w_out", [DEPTH, FF, D])
    W["ln_g"] = dt("ln_g", [DEPTH, 3, D])
    W["ln_b"] = dt("ln_b", [DEPTH, 3, D])
    ev_shapes = dict(w_in=[2, D, CO_END], conv_w=[2, 31, 512], conv_b=[2, 512], cln_g=[2, 512], cln_b=[2, 512], pe_k=[2, 32, 64],
                     w1_k=[2, 2048, 128], w2_k=[2, 128, 64], pe_v=[2, 32, 64], w1_v=[2, 2048, 128], w2_v=[2, 128, 64], w_out=[2, D, D])
    od_shapes = dict(a_re=[2, 64, 64], a_im=[2, 64, 64], log_dt=[2, 64], b_re=[2, 64, 64, 16], b_im=[2, 64, 64, 16],
                     c_re=[2, 64, 16, 64], c_im=[2, 64, 16, 64], d=[2, D], w_glu=[2, D, 2 * D])
    EV = {n: dt("ev_" + n, ev_shapes[n]) for n in EV_NAMES}
    OD = {n: dt("od_" + n, od_shapes[n]) for n in OD_NAMES}
    consts = dt("consts", [128, C_END])
    NCB = T // 16
    NCT = (NCB + 127) // 128
    ca_shapes = dict(c_qe=([8, 4, T], BF16), c_ke=([4, T], BF16), c_kce=([4, NCB], BF16), c_eall=([128, T], BF16),
                     c_ov=([128, NCT, 128], BF16), c_cmask=([128, 5, 512], BF16), c_tri2=([128, 2, 128], BF16),
                     c_wm=([128, 254], F32), c_wa=([128, 254], F32))
    CA = {n: dt(n, ca_shapes[n][0], ca_shapes[n][1]) for n in CA_NAMES}
    it = lambda n, sh, ty=F32: nc.dram_tensor(n, list(sh), ty, kind="Internal").ap()
    hbuf = [it("h_s%d" % i, [T, D]) for i in range(2)]
    hTbuf = [it("hT_s%d" % i, [T // 512, 128, 8, 512], BF16) for i in range(2)]
    SC = Ctx()
    SC.catT = it("catT", [T // 512, 128, 8, 512], BF16)
    SC.QA = it("QA", [8, 68, T], BF16)
    SC.KX = it("KX", [8, 64, T], BF16)
    SC.vtok = it("vtok", [T, 256], BF16)
    SC.gates = it("gates", [T, 24], F32)
    with ExitStack() as es:
        cx = setup_ctx(nc, es, consts[:, 0:128])
        S = cx.S
        S.dma("sp", SC.QA[:, 64:68, :], CA["c_qe"])
        wscr = [(it("wsin%d" % i, [D, 2 * FF], BF16), it("wsout%d" % i, [FF, D], BF16)) for i in range(2)]
        layers = list(layers)
        cast_ffn_weights(cx, W["ffn1_w_in"][layers[0]], W["ffn1_w_out"][layers[0]], wscr[0])
        xpose_phase(cx, x, hTbuf[0], T)
        h_cur, hT_cur, pp = x, hTbuf[0], 0

        def nxt():
            nonlocal pp
            pp ^= 1
            return hbuf[pp], hTbuf[pp]

        for li, layer in enumerate(layers):
            i = layer // 2
            ho, hTo = nxt()
            ffn_phase(cx, h_cur, hT_cur, wscr[0], W["ln_g"][layer, 0], W["ln_b"][layer, 0], ho, hTo, T)
            pre2 = lambda layer=layer: cast_ffn_weights(cx, W["ffn2_w_in"][layer], W["ffn2_w_out"][layer], wscr[1])
            if li + 1 < len(layers):
                nl = layers[li + 1]
                pre1 = lambda nl=nl: cast_ffn_weights(cx, W["ffn1_w_in"][nl], W["ffn1_w_out"][nl], wscr[0])
            else:
                pre1 = None
            h_cur, hT_cur = ho, hTo
            ho, hTo = nxt()
            if layer % 2 == 0:
                P = {n: EV[n][i] for n in EV_NAMES}
                e1_phase(cx, hT_cur, P, consts, SC, T, after_setup=pre2)
                e3_phase(cx, P, CA, SC, T)
                proj_phase(cx, h_cur, SC.catT, P["w_out"], D, False, W["ln_g"][layer, 1], W["ln_b"][layer, 1], ho, hTo, T, after_setup=pre1)
            else:
                P = {n: OD[n][i] for n in OD_NAMES}
                s5_phase(cx, hT_cur, P, consts, SC.catT, T, after_setup=pre2)
                proj_phase(cx, h_cur, SC.catT, P["w_glu"], 2 * D, True, W["ln_g"][layer, 1], W["ln_b"][layer, 1], ho, hTo, T, after_setup=pre1)
            h_cur, hT_cur = ho, hTo
            last = (li == len(layers) - 1)
            if last:
                ho, hTo = out, None
            else:
                ho, hTo = nxt()
            ffn_phase(cx, h_cur, hT_cur, wscr[1], W["ln_g"][layer, 2], W["ln_b"][layer, 2], ho, hTo, T)
            h_cur, hT_cur = ho, hTo
        S.final_wait("sp")
        nc._mk_stats = (S.n_inst, S.n_wait)
    return nc


_PROG = {}


def kernel(**inputs):
    x = np.asarray(inputs["x"], np.float32)
    B, T, _ = x.shape
    if T not in _PROG:
        _PROG[T] = build_program(T)
    nc = _PROG[T]
    shared = {k: np.ascontiguousarray(np.asarray(v, np.float32)) for k, v in inputs.items() if k != "x"}
    shared["consts"] = make_consts()
    shared.update(make_att_consts(T))
    workers = [0, 1, 4, 5][:B] if B <= 4 else list(range(B))
    MIRROR = False
    big = ["ffn1_w_in", "ffn1_w_out", "ffn2_w_in", "ffn2_w_out", "ev_w_in", "ev_w_out", "od_w_glu", "ev_w1_k", "ev_w1_v"]
    idle = dict(shared)
    for k in big:
        idle[k] = np.zeros_like(shared[k])
    idle["x"] = np.zeros((T, D), np.float32)
    in_maps = []
    for core in range(8):
        if core in workers:
            m = dict(shared)
            m["x"] = np.ascontiguousarray(x[workers.index(core)])
        elif MIRROR:
            m = dict(shared)
            m["x"] = np.ascontiguousarray(x[[2, 3, 6, 7].index(core) % B])
        else:
            m = idle
        in_maps.append(m)
    res = run_bass_kernel_spmd(nc, in_maps, core_ids=list(range(8)))
    return np.stack([np.asarray(res.results[workers[b]]["out"], np.float32) for b in range(B)], axis=0)
```
